# Optimizing a Trainium2 kernel written in Bass

```python
import jax, jax.numpy as jnp
from jax import lax
import numpy as np

D_MODEL = 1024
BATCH = 8
SEQ = 2048
DEPTH = 1

CTX_LEN = 256
GRID_W = 64

H_A = 8
DN_A = 64
DR_A = 32
DV_A = 64
W_A = H_A * DV_A
Q_LORA = 256
KV_LORA = 128
ROPE_FREQS = DR_A // 4
ROPE_BASE = 10000.0
Q_BLOCK = 128

H_B = 4
DH_B = 128
W_B = H_B * DH_B
QKV_BS = 4
N_QKV_BLK = W_B // QKV_BS
CONV_W = 3
CHUNK = 64

W_MIX = W_A + W_B
N_IN = Q_LORA + KV_LORA + DR_A + W_A + 3 * W_B
SPLITS = (Q_LORA,
          Q_LORA + KV_LORA,
          Q_LORA + KV_LORA + DR_A,
          Q_LORA + KV_LORA + DR_A + W_A,
          Q_LORA + KV_LORA + DR_A + W_A + W_B,
          Q_LORA + KV_LORA + DR_A + W_A + 2 * W_B)
ALPHA = (2.0 * DEPTH) ** 0.25
BETA = (8.0 * DEPTH) ** -0.25
LN_EPS = 1e-5
RMS_EPS = 1e-6

kernel_name = "hymba_mla_mlstm_prefix_dit_layer"


def layer_norm(x, eps=LN_EPS):
    xf = x.astype(jnp.float32)
    xc = xf - xf.mean(-1, keepdims=True)
    var = jnp.mean(xc * xc, -1, keepdims=True)
    return (xc * lax.rsqrt(var + eps)).astype(x.dtype)


def rms_norm(x, g, eps=RMS_EPS):
    xf = x.astype(jnp.float32)
    return (xf * lax.rsqrt(jnp.mean(xf * xf, -1, keepdims=True) + eps)).astype(x.dtype) * g


def axial_rope_tables(seq):
    n_rows = seq // GRID_W
    row = jnp.repeat(jnp.arange(n_rows, dtype=jnp.float32), GRID_W)
    col = jnp.tile(jnp.arange(GRID_W, dtype=jnp.float32), n_rows)
    inv = ROPE_BASE ** (-jnp.arange(ROPE_FREQS, dtype=jnp.float32) / ROPE_FREQS)
    ang = jnp.stack([row[:, None] * inv, col[:, None] * inv], axis=1)
    return jnp.cos(ang), jnp.sin(ang)


def rope2d(x, cos, sin):
    xs = x.reshape(x.shape[:-1] + (2, 2, ROPE_FREQS)).astype(jnp.float32)
    x1, x2 = xs[..., 0, :], xs[..., 1, :]
    out = jnp.stack([x1 * cos - x2 * sin, x2 * cos + x1 * sin], axis=-2)
    return out.reshape(x.shape).astype(x.dtype)


def mla_project(q_a, kv_a, k_r, g_qa, w_qb, g_kva, w_kvb):
    b, t, _ = q_a.shape
    q = (rms_norm(q_a, g_qa) @ w_qb).reshape(b, t, H_A, DN_A + DR_A)
    kv = (rms_norm(kv_a, g_kva) @ w_kvb).reshape(b, t, H_A, DN_A + DV_A)
    return q[..., :DN_A], q[..., DN_A:], kv[..., :DN_A], k_r, kv[..., DN_A:]


def block_attention(qn, qr, kn, kr, v):
    b, tq, h, _ = qn.shape
    nb = tq // Q_BLOCK
    scale = (DN_A + DR_A) ** -0.5
    qn_b = qn.reshape(b, nb, Q_BLOCK, h, DN_A).swapaxes(0, 1)
    qr_b = qr.reshape(b, nb, Q_BLOCK, h, DR_A).swapaxes(0, 1)

    def one_block(args):
        qn_i, qr_i = args
        s = jnp.einsum('bqhd,bkhd->bhqk', qn_i, kn) + jnp.einsum('bqhr,bkr->bhqk', qr_i, kr)
        p = jax.nn.softmax(s.astype(jnp.float32) * scale, axis=-1).astype(v.dtype)
        return jnp.einsum('bhqk,bkhd->bqhd', p, v)

    o = lax.map(one_block, (qn_b, qr_b))
    return o.swapaxes(0, 1).reshape(b, tq, h * DV_A)


def dwconv_centred(x, w, bias):
    t = x.shape[1]
    pad = CONV_W // 2
    xp = jnp.pad(x, ((0, 0), (pad, pad), (0, 0)))
    out = bias
    for j in range(CONV_W):
        out = out + xp[:, j:j + t] * w[j]
    return out


def headwise(x, w):
    b, t, _ = x.shape
    nb, bs, _ = w.shape
    return jnp.einsum('btgi,gio->btgo', x.reshape(b, t, nb, bs), w).reshape(b, t, nb * bs)


def to_heads(x):
    b, t, _ = x.shape
    return x.reshape(b, t, H_B, DH_B).transpose(0, 2, 1, 3)


def mlstm_inputs(x_m, conv_w, conv_b, w_mq, w_mk, w_mv, w_gate, b_gate):
    xc = jax.nn.silu(dwconv_centred(x_m, conv_w, conv_b))
    q = headwise(xc, w_mq)
    k = headwise(xc, w_mk)
    v = headwise(x_m, w_mv)
    g = (jnp.concatenate([q, k, v], -1) @ w_gate + b_gate).astype(jnp.float32)
    g = g.reshape(g.shape[0], g.shape[1], 4, H_B).transpose(2, 0, 3, 1)
    gates_f = (g[0], jax.nn.log_sigmoid(g[1]))
    gates_b = (g[2], jax.nn.log_sigmoid(g[3]))
    return xc, to_heads(q), to_heads(k * DH_B ** -0.5), to_heads(v), gates_f, gates_b


def zero_state(b):
    return (jnp.zeros((b, H_B, DH_B, DH_B), jnp.float32),
            jnp.zeros((b, H_B, DH_B), jnp.float32),
            jnp.zeros((b, H_B), jnp.float32))


def mlstm_final_state(k, v, li, lf, state):
    c0, n0, m0 = state
    bcum = jnp.cumsum(lf, axis=-1)
    b_end = bcum[..., -1]
    w = b_end[..., None] - bcum + li
    m = jnp.maximum(b_end + m0, w.max(-1))
    a0 = jnp.exp(b_end + m0 - m)
    ws = jnp.exp(w - m[..., None])
    c_new = a0[..., None, None] * c0 + jnp.einsum('bhs,bhsv,bhsd->bhvd', ws, v, k)
    n_new = a0[..., None] * n0 + jnp.einsum('bhs,bhsd->bhd', ws, k)
    return (c_new, n_new, m)


def mlstm_chunkwise(q, k, v, li, lf, state):
    b, h, t, _ = q.shape
    nc = t // CHUNK

    def split(a):
        return jnp.moveaxis(a.reshape(a.shape[:2] + (nc, CHUNK) + a.shape[3:]), 2, 0)

    mask = jnp.tril(jnp.ones((CHUNK, CHUNK), dtype=bool))

    def step(st, inp):
        c0, n0, m0 = st
        qc, kc, vc, lic, lfc = inp
        bcum = jnp.cumsum(lfc, axis=-1)
        dmat = jnp.where(mask, bcum[..., :, None] - bcum[..., None, :] + lic[..., None, :], -jnp.inf)
        inter = bcum + m0[..., None]
        m = jnp.maximum(inter, dmat.max(-1))
        a_inter = jnp.exp(inter - m)
        s = jnp.einsum('bhtd,bhsd->bhts', qc, kc) * jnp.exp(dmat - m[..., None])
        num = a_inter[..., None] * jnp.einsum('bhvd,bhtd->bhtv', c0, qc) + jnp.einsum('bhts,bhsv->bhtv', s, vc)
        den = a_inter * jnp.einsum('bhd,bhtd->bht', n0, qc) + s.sum(-1)
        h_chunk = num / jnp.maximum(jnp.abs(den), jnp.exp(-m))[..., None]
        return mlstm_final_state(kc, vc, lic, lfc, st), h_chunk

    st, hs = lax.scan(step, state, tuple(split(a) for a in (q, k, v, li, lf)))
    return jnp.moveaxis(hs, 0, 2).reshape(b, h, t, v.shape[-1]), st


def flip_t(a):
    return jnp.flip(a, axis=2)


def stream_tensors(u, w_in, g_qa, w_qb, g_kva, w_kvb, conv_w, conv_b, w_mq, w_mk, w_mv, w_gate, b_gate):
    p = u @ w_in
    q_a, kv_a, k_r, z_a, x_m, o_m, z_m = jnp.split(p, SPLITS, axis=-1)
    mla = mla_project(q_a, kv_a, k_r, g_qa, w_qb, g_kva, w_kvb)
    mls = mlstm_inputs(x_m, conv_w, conv_b, w_mq, w_mk, w_mv, w_gate, b_gate)
    return mla, mls, (z_a, o_m, z_m)


def mixer_output(att, z_a, h_cell, o_m, xc, z_m, mh_g, skip, w_out):
    b, _, t, _ = h_cell.shape
    hb = jax.nn.sigmoid(o_m) * h_cell.transpose(0, 2, 1, 3).reshape(b, t, W_B).astype(o_m.dtype)
    hb = layer_norm(hb.reshape(b, t, H_B, DH_B)).reshape(b, t, W_B) * mh_g + skip * xc
    y_a = att * jax.nn.silu(z_a)
    y_b = hb * jax.nn.silu(z_m)
    return jnp.concatenate([y_a, y_b], axis=-1) @ w_out


def hybrid_layer(h, hc, c, c_ctx, cos, sin, w_ada, b_ada, w_in, g_qa, w_qb, g_kva, w_kvb,
                 conv_w, conv_b, w_mq, w_mk, w_mv, w_gate, b_gate, mh_g, skip, w_out,
                 ln_g, ln_b, update_ctx):
    shift, scale, gate = jnp.split((jax.nn.silu(c) @ w_ada + b_ada)[:, None, :], 3, axis=-1)
    shift_c, scale_c, gate_c = jnp.split(jax.nn.silu(c_ctx) @ w_ada + b_ada, 3, axis=-1)
    weights = (w_in, g_qa, w_qb, g_kva, w_kvb, conv_w, conv_b, w_mq, w_mk, w_mv, w_gate, b_gate)

    (qn, qr, kn, kr, v), (xc, q, k, vm, gf, gb), (z_a, o_m, z_m) = stream_tensors(
        h * (1 + scale) + shift, *weights)
    (qn_c, qr_c, kn_c, kr_c, v_c), (xc_c, q_c, k_c, vm_c, gf_c, gb_c), (z_a_c, o_m_c, z_m_c) = stream_tensors(
        hc * (1 + scale_c) + shift_c, *weights)

    qr = rope2d(qr, cos[:, None], sin[:, None])
    kr = rope2d(kr, cos, sin)
    att = block_attention(qn, qr,
                          jnp.concatenate([kn_c, kn], axis=1),
                          jnp.concatenate([kr_c, kr], axis=1),
                          jnp.concatenate([v_c, v], axis=1))

    st0 = zero_state(h.shape[0])
    if update_ctx:
        h_cf, st_f = mlstm_chunkwise(q_c, k_c, vm_c, *gf_c, st0)
        h_cb, st_b = mlstm_chunkwise(*map(flip_t, (q_c, k_c, vm_c, *gb_c)), st0)
    else:
        st_f = mlstm_final_state(k_c, vm_c, *gf_c, st0)
        st_b = mlstm_final_state(*map(flip_t, (k_c, vm_c, *gb_c)), st0)
    h_f, _ = mlstm_chunkwise(q, k, vm, *gf, st_f)
    h_b, _ = mlstm_chunkwise(*map(flip_t, (q, k, vm, *gb)), st_b)

    y = mixer_output(att, z_a, h_f + flip_t(h_b), o_m, xc, z_m, mh_g, skip, w_out)
    h_new = layer_norm(ALPHA * h + gate * y) * ln_g + ln_b

    if update_ctx:
        att_c = block_attention(qn_c, qr_c, kn_c, kr_c, v_c)
        y_c = mixer_output(att_c, z_a_c, h_cf + flip_t(h_cb), o_m_c, xc_c, z_m_c, mh_g, skip, w_out)
        hc = layer_norm(ALPHA * hc + gate_c * y_c) * ln_g + ln_b
    return h_new, hc


def setup_inputs(seed: int = 0) -> dict:
    key = jax.random.key(seed)
    ks = jax.random.split(key, 32)
    L = DEPTH
    f32 = jnp.float32

    def nrm(k, shape, s):
        return jax.random.normal(k, shape, f32) * s

    f_bias = jnp.linspace(3.0, 6.0, H_B, dtype=f32)
    b_gate = jnp.concatenate([nrm(ks[20], (L, H_B), 0.1),
                              f_bias + nrm(ks[21], (L, H_B), 0.1),
                              nrm(ks[22], (L, H_B), 0.1),
                              f_bias + nrm(ks[23], (L, H_B), 0.1)], axis=-1)
    return {
        "x": nrm(ks[0], (BATCH, SEQ, D_MODEL), 1.0),
        "c": nrm(ks[1], (BATCH, D_MODEL), 1.0),
        "ctx": nrm(ks[2], (BATCH, CTX_LEN, D_MODEL), 1.0),
        "c_ctx": nrm(ks[3], (D_MODEL,), 1.0),
        "ln_in_g": 1.0 + nrm(ks[4], (D_MODEL,), 0.02),
        "ln_in_b": nrm(ks[5], (D_MODEL,), 0.02),
        "w_ada": nrm(ks[6], (L, D_MODEL, 3 * D_MODEL), 0.5 * D_MODEL ** -0.5),
        "b_ada": nrm(ks[7], (L, 3 * D_MODEL), 0.02),
        "w_in": nrm(ks[8], (L, D_MODEL, N_IN), D_MODEL ** -0.5),
        "g_qa": 1.0 + nrm(ks[9], (L, Q_LORA), 0.02),
        "w_qb": nrm(ks[10], (L, Q_LORA, H_A * (DN_A + DR_A)), Q_LORA ** -0.5),
        "g_kva": 1.0 + nrm(ks[11], (L, KV_LORA), 0.02),
        "w_kvb": nrm(ks[12], (L, KV_LORA, H_A * (DN_A + DV_A)), KV_LORA ** -0.5),
        "conv_w": nrm(ks[13], (L, CONV_W, W_B), CONV_W ** -0.5),
        "conv_b": nrm(ks[14], (L, W_B), 0.02),
        "w_mq": nrm(ks[15], (L, N_QKV_BLK, QKV_BS, QKV_BS), QKV_BS ** -0.5),
        "w_mk": nrm(ks[16], (L, N_QKV_BLK, QKV_BS, QKV_BS), QKV_BS ** -0.5),
        "w_mv": nrm(ks[17], (L, N_QKV_BLK, QKV_BS, QKV_BS), QKV_BS ** -0.5),
        "w_gate": nrm(ks[18], (L, 3 * W_B, 4 * H_B), (3 * W_B) ** -0.5),
        "b_gate": b_gate,
        "mh_g": 1.0 + nrm(ks[24], (L, W_B), 0.02),
        "skip": 1.0 + nrm(ks[25], (L, W_B), 0.02),
        "w_out": nrm(ks[26], (L, W_MIX, D_MODEL), BETA * W_MIX ** -0.5),
        "ln_g": 1.0 + nrm(ks[27], (L, D_MODEL), 0.02),
        "ln_b": nrm(ks[28], (L, D_MODEL), 0.02),
    }


def reference(x, c, ctx, c_ctx, ln_in_g, ln_in_b, w_ada, b_ada, w_in, g_qa, w_qb, g_kva, w_kvb,
              conv_w, conv_b, w_mq, w_mk, w_mv, w_gate, b_gate, mh_g, skip, w_out, ln_g, ln_b):
    cos, sin = axial_rope_tables(x.shape[1])
    h = layer_norm(x) * ln_in_g + ln_in_b
    hc = layer_norm(ctx) * ln_in_g + ln_in_b
    for l in range(DEPTH):
        h, hc = hybrid_layer(h, hc, c, c_ctx, cos, sin, w_ada[l], b_ada[l], w_in[l], g_qa[l], w_qb[l],
                             g_kva[l], w_kvb[l], conv_w[l], conv_b[l], w_mq[l], w_mk[l], w_mv[l],
                             w_gate[l], b_gate[l], mh_g[l], skip[l], w_out[l], ln_g[l], ln_b[l],
                             update_ctx=(l < DEPTH - 1))
    return h
```

```python
import numpy as np
from contextlib import ExitStack
import concourse.bass as bass
import concourse.mybir as mybir
from concourse.bass_utils import run_bass_kernel_spmd

F32 = mybir.dt.float32
BF16 = mybir.dt.bfloat16
AF = mybir.ActivationFunctionType
ALU = mybir.AluOpType

N_DMA_SEMS = 24

D = 1024
T_LAT = 2048
T_CTX = 256
T_ALL = T_LAT + T_CTX
NT = T_ALL // 128
LN_EPS = 1e-5
RMS_EPS = 1e-6
ALPHA = 2.0 ** 0.25
ATT_SCALE = 96.0 ** -0.5
NV = 67
V_GIN, V_BIN, V_BADA, V_CW, V_CB, V_MHG, V_SKIP, V_GQA, V_GKVA = 0, 8, 16, 40, 52, 56, 60, 64, 66
C_QA, C_KVA, C_KR, C_ZA, C_XM, C_OM, C_ZM = 0, 256, 384, 416, 928, 1440, 1952


class Prog:
    def __init__(self, nc, es):
        self.nc = nc
        self.es = es
        self.ops = []
        self.acc = {}
        self.dma_rr = {"pool": 0, "hw": 0}
        self.dma_rng = {"pool": (0, 8), "hw": (8, N_DMA_SEMS)}
        self.dma_last = [None] * N_DMA_SEMS
        self.dma_cnt = [0] * N_DMA_SEMS
        self.barrier_op = None
        self.last_eng = {}
        self.open_dma = []
        self.flushed = 0
        self.engs = ["pe", "act", "dve", "pool", "sp"]
        self.esem = {e: es.enter_context(nc.semaphore("s_" + e)) for e in self.engs}
        self.dsem = [es.enter_context(nc.semaphore("d_%d" % k)) for k in range(N_DMA_SEMS)]
        self.cnt = {e: 0 for e in self.engs}
        self.seen = {e: {} for e in self.engs}
        self.nwaits = 0

    @staticmethod
    def _conf(a, b):
        n = min(len(a), len(b))
        return a[:n] == b[:n]

    def add(self, eng, fn, r=(), w=(), dma=False, sticky=False):
        i = len(self.ops)
        deps = set()
        r = [k if isinstance(k, tuple) else (k,) for k in r]
        w = [k if isinstance(k, tuple) else (k,) for k in w]
        for k in list(r):
            if k[0].startswith("ps"):
                r.remove(k)
                w.append((k[0],))
        w = [(k[0],) if k[0].startswith("ps") else k for k in w]
        for k in r:
            d = self.acc.setdefault(k[0], {})
            for sk, st in d.items():
                if self._conf(sk, k[1:]) and st[0] is not None:
                    deps.add(st[0])
        for k in w:
            d = self.acc.setdefault(k[0], {})
            for sk, st in d.items():
                if self._conf(sk, k[1:]):
                    if st[0] is not None:
                        deps.add(st[0])
                    deps.update(st[1])
        for k in r:
            d = self.acc[k[0]]
            st = d.setdefault(k[1:], [None, []])
            st[1].append(i)
        for k in w:
            d = self.acc[k[0]]
            for sk in [sk for sk in d if len(sk) >= len(k[1:]) and self._conf(sk, k[1:])]:
                del d[sk]
            d[k[1:]] = [i, []]
        if self.barrier_op is not None:
            deps.add(self.barrier_op)
        deps.discard(i)
        self.last_eng[eng] = i
        if dma and not sticky:
            self.open_dma.append(i)
        op = dict(eng=eng, fn=fn, deps=deps, dma=dma, sig=False, sem=None, val=None, sticky=sticky)
        if dma:
            cls = "pool" if eng == "pool" else "hw"
            lo, hi = self.dma_rng[cls]
            s = lo + self.dma_rr[cls]
            self.dma_rr[cls] = (self.dma_rr[cls] + 1) % (hi - lo)
            if self.dma_last[s] is not None:
                op["deps"].add(self.dma_last[s])
            self.dma_last[s] = i
            self.dma_cnt[s] += 16
            op["dsem"] = s
            op["val"] = self.dma_cnt[s]
        self.ops.append(op)
        return i

    def barrier(self):
        deps = set(self.last_eng.values()) | set(self.open_dma)
        self.open_dma = []
        i = self.add("pool", lambda e: e.nop())
        self.ops[i]["deps"].update(d for d in deps if d != i)
        self.ops[i]["sig"] = True
        self.ops[i]["is_bar"] = True
        self.barrier_op = i
        self.flush()
        return i

    def flush(self):
        nc = self.nc
        ops = self.ops
        base = self.flushed
        engs = self.engs
        eobj = dict(pe=nc.tensor, act=nc.scalar, dve=nc.vector, pool=nc.gpsimd, sp=nc.sync)
        for i in range(base, len(ops)):
            op = ops[i]
            op["deps"] = {d for d in op["deps"] if d >= base or ops[d].get("is_bar") or ops[d].get("sticky")}
            for d in op["deps"]:
                dop = ops[d]
                if dop["dma"]:
                    continue
                if dop["eng"] == "pe" and op["eng"] == "pe" and not op["dma"]:
                    continue
                dop["sig"] = True
        for i in range(base, len(ops)):
            op = ops[i]
            if op["dma"]:
                op["sem"] = self.dsem[op["dsem"]]
            elif op["sig"]:
                self.cnt[op["eng"]] += 1
                op["sem"] = self.esem[op["eng"]]
                op["val"] = self.cnt[op["eng"]]
        per = {e: [] for e in engs}
        for i in range(base, len(ops)):
            per[ops[i]["eng"]].append(i)

        def run(e):
            eng = eobj[e]
            seen = self.seen[e]
            for i in per[e]:
                op = ops[i]
                need = {}
                for d in op["deps"]:
                    dop = ops[d]
                    if (not dop["dma"]) and dop["eng"] == "pe" and e == "pe" and not op["dma"]:
                        continue
                    sem = dop["sem"]
                    key = id(sem)
                    if dop["val"] > need.get(key, (None, 0))[1]:
                        need[key] = (sem, dop["val"])
                for key, (sem, val) in need.items():
                    if seen.get(key, 0) >= val:
                        continue
                    eng.wait_ge(sem, val)
                    self.nwaits += 1
                    seen[key] = val
                ins = op["fn"](eng)
                if op["dma"]:
                    ins.then_inc(op["sem"], 16)
                elif op["sig"]:
                    ins.then_inc(op["sem"], 1)
                op["fn"] = None

        with nc.Block() as block:
            @block.tensor
            def _(e):
                run("pe")

            @block.scalar
            def _(e):
                run("act")

            @block.vector
            def _(e):
                run("dve")

            @block.gpsimd
            def _(e):
                run("pool")

            @block.sync
            def _(e):
                run("sp")
        self.flushed = len(ops)

    def emit(self):
        if self.flushed < len(self.ops):
            self.barrier()
        return dict(n_ops=len(self.ops), n_waits=self.nwaits, sig=dict(self.cnt))


def build_nc(stage=99, dbg=None):
    dbg = dbg or {}
    nc = bass.Bass("TRN2", target_bir_lowering=False)
    es = ExitStack()
    P = Prog(nc, es)

    def dram(name, shape, dt=F32, out=False):
        return nc.dram_tensor(name, list(shape), dt, kind="ExternalOutput" if out else "ExternalInput").ap()

    x_d = dram("x", [T_LAT, D])
    ctx_d = dram("ctx", [T_CTX, D])
    cc_d = dram("ccT", [128, 8, 2])
    vecs_d = dram("vecs", [128, NV])
    wada_d = dram("w_ada", [D, 3 * D])
    win_d = dram("w_in", [D, 2464])
    wqb_d = dram("w_qb", [256, 768])
    wkvb_d = dram("w_kvb", [128, 1024])
    wout_d = dram("w_out", [D, D])
    wgate_d = dram("w_gate", [1536, 16])
    bgate_d = dram("b_gate", [1, 16])
    wm_d = [dram(n, [512, 4]) for n in ("w_mq", "w_mk", "w_mv")]
    rows_d = dram("rows", [4, D])
    rope_d = dram("rope", [2, 32, T_LAT])
    out_d = dram("out", [T_LAT, D], out=True)
    dbg_d = {k: dram(k, shp, out=True) for k, shp in dbg.items()}

    def sbx(stack, name, shape, dt=F32):
        return stack.enter_context(nc.sbuf_tensor(name, list(shape), dt))

    def sb(name, shape, dt=F32):
        return sbx(es, name, shape, dt)

    def MM(out, lhsT, rhs, start=True, stop=True, r=(), w=()):
        P.add("pe", lambda e: e.matmul(out, lhsT=lhsT, rhs=rhs, start=start, stop=stop), r=r, w=w)

    def ACT(out, in_, func, r=(), w=(), scale=1.0, bias=None):
        if bias is None:
            P.add("act", lambda e: e.activation(out=out, in_=in_, func=func, scale=scale), r=r, w=w)
        else:
            P.add("act", lambda e: e.activation(out=out, in_=in_, func=func, scale=scale, bias=bias), r=r, w=w)

    def TS(eng, out, in0, s1, s2, op0, op1=None, r=(), w=()):
        if op1 is None:
            P.add(eng, lambda e: e.tensor_scalar(out=out, in0=in0, scalar1=s1, scalar2=None, op0=op0), r=r, w=w)
        else:
            P.add(eng, lambda e: e.tensor_scalar(out=out, in0=in0, scalar1=s1, scalar2=s2, op0=op0, op1=op1), r=r, w=w)

    def TT(eng, out, in0, in1, op, r=(), w=()):
        P.add(eng, lambda e: e.tensor_tensor(out=out, in0=in0, in1=in1, op=op), r=r, w=w)

    def STT(eng, out, in0, scalar, in1, op0, op1, r=(), w=()):
        P.add(eng, lambda e: e.scalar_tensor_tensor(out=out, in0=in0, scalar=scalar, in1=in1, op0=op0, op1=op1),
              r=r, w=w)

    def CP(eng, out, in_, r=(), w=()):
        if eng == "act":
            P.add("act", lambda e: e.activation(out=out, in_=in_, func=AF.Copy), r=r, w=w)
        else:
            P.add(eng, lambda e: e.tensor_copy(out=out, in_=in_), r=r, w=w)

    def MSET(eng, ap, val, w=()):
        P.add(eng, lambda e: e.memset(ap, val), w=w)

    def RSQRT(out, in_, eps, r=(), w=()):
        ACT(out, in_, AF.Ln, r=r, w=w, bias=eps)
        ACT(out, out, AF.Exp, r=w, w=w, scale=-0.5)

    def RECIP(out, in_, r=(), w=()):
        ACT(out, in_, AF.Ln, r=r, w=w)
        ACT(out, out, AF.Exp, r=w, w=w, scale=-1.0)

    def DMA(q, out, in_, r=(), w=(), sticky=False):
        P.add(q, lambda e: e.dma_start(out=out, in_=in_), r=r, w=w, dma=True, sticky=sticky)

    def silu_from(ps_ap, ps_key, out_ap, out_key, shape, tmp_th, tmp_zz, bias_full=None, bias_half=None,
                  kth="tmp_th", kzz="tmp_zz", zz_on_act=False):
        if bias_half is None:
            ACT(tmp_th, ps_ap, AF.Tanh, r=[ps_key], w=[kth], scale=0.5)
            TS("dve", tmp_zz, ps_ap, 0.5, None, ALU.mult, r=[ps_key], w=[kzz])
        elif zz_on_act:
            ACT(tmp_th, ps_ap, AF.Tanh, r=[ps_key], w=[kth], scale=0.5, bias=bias_half)
            ACT(tmp_zz, ps_ap, AF.Identity, r=[ps_key], w=[kzz], scale=0.5, bias=bias_half)
        else:
            ACT(tmp_th, ps_ap, AF.Tanh, r=[ps_key], w=[kth], scale=0.5, bias=bias_half)
            TS("dve", tmp_zz, ps_ap, bias_full, 0.5, ALU.add, ALU.mult, r=[ps_key], w=[kzz])
        STT("dve", out_ap, tmp_th, 1.0, tmp_zz, ALU.add, ALU.mult, r=[kth, kzz], w=[out_key])

    def ps(name, shape, dt=F32):
        return es.enter_context(nc.psum_tensor(name, list(shape), dt))

    psA = [ps("psA%d" % i, [128, 512]) for i in range(3)]
    psO = ps("psO", [128, 512])
    psB = ps("psB", [128, 512])
    psC = ps("psC", [128, 512])
    psD = ps("psD", [128, 512])
    psE = ps("psE", [128, 512])

    ident_f = sb("ident_f", [128, 128])
    ident_b = sb("ident_b", [128, 128], BF16)
    ones_f = sb("ones_f", [128, 128])
    ones_b = sb("ones_b", [128, 128], BF16)
    vecs = sb("vecs_sb", [128, NV])
    tri_f = sb("tri_f", [128, 2, 128])
    ada = sb("ada", [128, 24, 2])
    AB = sb("AB", [128, 2, 2, 8])
    uT = sb("uT", [128, 8, T_ALL], BF16)
    stats = sb("stats", [128, NT, 2])
    ycatT = sb("ycatT", [128, 8, T_LAT], BF16)
    MSET("pool", ones_f[:], 1.0, w=["ones_f"])
    MSET("pool", ones_b[:], 1.0, w=["ones_b"])
    P.add("pool", lambda e: e.affine_select(out=ident_f[:], in_=ones_f[:], pattern=[[-1, 128]],
                                            compare_op=ALU.is_equal, fill=0.0, base=0, channel_multiplier=1),
          r=["ones_f"], w=["ident_f"])
    CP("pool", ident_b[:], ident_f[:], r=["ident_f"], w=["ident_b"])
    P.add("pool", lambda e: e.affine_select(out=tri_f[:, 0, :], in_=ones_f[:], pattern=[[1, 128]],
                                            compare_op=ALU.is_ge, fill=0.0, base=0, channel_multiplier=-1),
          r=["ones_f"], w=[("tri_f", 0)])
    P.add("pool", lambda e: e.affine_select(out=tri_f[:, 1, :], in_=ones_f[:], pattern=[[-1, 128]],
                                            compare_op=ALU.is_ge, fill=0.0, base=0, channel_multiplier=1),
          r=["ones_f"], w=[("tri_f", 1)])
    DMA("sp", vecs[:], vecs_d, w=["vecs"])

    win_v = win_d.rearrange("(c p) n -> p c n", p=128)
    sw = ExitStack()
    wom = sbx(sw, "wom", [128, 8, 512], BF16)
    wzm = sbx(sw, "wzm", [128, 8, 512], BF16)
    st0 = ExitStack()
    cc = sbx(st0, "cc", [128, 8, 2])
    th0 = sbx(st0, "th0", [128, 8, 2])
    scc = sbx(st0, "scc", [128, 8, 2], BF16)
    wada = [sbx(st0, "wada%d" % i, [128, 8, 1024]) for i in range(2)]
    wadab = [sbx(st0, "wadab%d" % i, [128, 8, 1024], BF16) for i in range(2)]
    t1 = sbx(st0, "t1", [128, 8])
    NXB = 3
    xt = [sbx(st0, "xt%d" % i, [128, D]) for i in range(NXB)]
    xn = [sbx(st0, "xn%d" % i, [128, D], BF16) for i in range(NXB)]
    st6 = [sbx(st0, "st6_%d" % i, [128, 2, 6]) for i in range(2)]
    mv = [sbx(st0, "mv%d" % i, [128, 2]) for i in range(2)]
    wada_v = wada_d.rearrange("(c p) n -> p c n", p=128)
    ps_ada = psB[:, 0:48].rearrange("p (a b) -> p a b", b=2)
    ps_t = [psC[:].bitcast(BF16).rearrange("p (c t) -> p c t", t=128),
            psD[:].bitcast(BF16).rearrange("p (c t) -> p c t", t=128)]
    ps_tk = ["psC", "psD"]
    ps_u = [psE[:].bitcast(BF16).rearrange("p (c t) -> p c t", t=128),
            psO[:].bitcast(BF16).rearrange("p (c t) -> p c t", t=128)]
    ps_uk = ["psE", "psO"]
    NACT = 6

    def ada_load(part):
        DMA("sp", wada[part % 2][:], wada_v[:, :, part * 1024:(part + 1) * 1024], w=[("wada", part % 2)])

    def ada_mm(part):
        pb = part % 2
        CP("dve", wadab[pb][:, 0:4, :], wada[pb][:, 0:4, :], r=[("wada", pb)], w=[("wadab", pb, 0)])
        CP("act", wadab[pb][:, 4:8, :], wada[pb][:, 4:8, :], r=[("wada", pb)], w=[("wadab", pb, 1)])
        for n in range(8):
            for k in range(8):
                MM(ps_ada[:, part * 8 + n, :], wadab[pb][:, k, n * 128:(n + 1) * 128], scc[:, k, :],
                   start=(k == 0), stop=(k == 7), r=[("wadab", pb), "scc"], w=["psB"])
        for j in range(2):
            TT("dve", ada[:, part * 8:(part + 1) * 8, j], ps_ada[:, part * 8:(part + 1) * 8, j],
               vecs[:, V_BADA + part * 8:V_BADA + (part + 1) * 8], ALU.add, r=["psB", "vecs"], w=[("ada", part, j)])

    def ln_a(i):
        b = i % NXB
        s2 = i % 2
        src = ctx_d[i * 128:(i + 1) * 128, :] if i < 2 else x_d[(i - 2) * 128:(i - 1) * 128, :]
        DMA("sp", xt[b][:], src, w=[("xt", b)])
        for hh in range(2):
            P.add("dve", lambda e, b=b, hh=hh, s2=s2: e.bn_stats(out=st6[s2][:, hh, :],
                                                               in_=xt[b][:, hh * 512:(hh + 1) * 512]),
                  r=[("xt", b)], w=[("st6", s2, hh)])
        P.add("dve", lambda e, s2=s2: e.bn_aggr(out=mv[s2][:], in_=st6[s2][:]), r=[("st6", s2)], w=[("mv", s2)])
        CP("dve", stats[:, i, 0:1], mv[s2][:, 0:1], r=[("mv", s2)], w=[("stats", i)])
        RSQRT(stats[:, i, 1:2], mv[s2][:, 1:2], LN_EPS, r=[("mv", s2)], w=[("stats", i)])

    def ln_a2(i):
        b = i % NXB
        TS("dve", xn[b][:], xt[b][:], stats[:, i, 0:1], stats[:, i, 1:2], ALU.subtract, ALU.mult,
           r=[("xt", b), ("stats", i)], w=[("xn", b)])

    def ln_b(i):
        b = i % NXB
        pb = i % 2
        j = 1 if i < 2 else 0
        for c in range(8):
            tgt, tk = (ps_t[pb], ps_tk[pb]) if c < NACT else (ps_u[pb], ps_uk[pb])
            P.add("pe", lambda e, b=b, c=c, tgt=tgt: e.transpose(out=tgt[:, c, :],
                                                                in_=xn[b][:, c * 128:(c + 1) * 128],
                                                                identity=ident_b[:]),
                  r=[("xn", b), "ident_b"], w=[tk])
        for c in range(8):
            if c < NACT:
                ACT(uT[:, c, i * 128:(i + 1) * 128], ps_t[pb][:, c, :], AF.Identity, r=[ps_tk[pb], ("AB", j)],
                    w=[("uT", c, i)], scale=AB[:, j, 0, c:c + 1], bias=AB[:, j, 1, c:c + 1])
            else:
                TS("dve", uT[:, c, i * 128:(i + 1) * 128], ps_u[pb][:, c, :], AB[:, j, 0, c:c + 1],
                   AB[:, j, 1, c:c + 1], ALU.mult, ALU.add, r=[ps_uk[pb], ("AB", j)], w=[("uT", c, i)])

    DMA("sp", cc[:], cc_d, w=["cc"])
    ada_load(0)
    ada_load(1)
    ACT(th0[:], cc[:], AF.Tanh, r=["cc"], w=["th0"], scale=0.5)
    TS("dve", th0[:], th0[:], 0.5, 0.5, ALU.mult, ALU.add, r=["th0"], w=["th0"])
    TT("dve", scc[:], th0[:], cc[:], ALU.mult, r=["th0", "cc"], w=["scc"])
    ln_a(0)
    ln_a2(0)
    ln_a(1)
    ln_a2(1)
    ada_mm(0)
    ada_mm(1)
    DMA("pool", wom[:], win_v[:, :, C_OM:C_OM + 512], w=["wom"], sticky=True)
    DMA("pool", wzm[:], win_v[:, :, C_ZM:C_ZM + 512], w=["wzm"], sticky=True)
    for j in range(2):
        TS("dve", t1[:], ada[:, 8:16, j], 1.0, None, ALU.add, r=[("ada", 1, j)], w=["t1"])
        TT("dve", AB[:, j, 0, :], t1[:], vecs[:, V_GIN:V_GIN + 8], ALU.mult, r=["t1", "vecs"], w=[("AB", j, 0)])
        TT("dve", AB[:, j, 1, :], t1[:], vecs[:, V_BIN:V_BIN + 8], ALU.mult, r=["t1", "vecs"], w=[("AB", j, 1)])
        TT("dve", AB[:, j, 1, :], AB[:, j, 1, :], ada[:, 0:8, j], ALU.add,
           r=[("AB", j, 1), ("ada", 0, j)], w=[("AB", j, 1)])
    for i in range(NT):
        if i + 2 < NT:
            ln_a(i + 2)
        ln_b(i)
        if i + 2 < NT:
            ln_a2(i + 2)
        if i == NT - 3:
            ada_load(2)
    ada_mm(2)
    P.barrier()
    st0.close()

    blocks = [(0, 256)] + [(256 + 512 * j, 512) for j in range(4)]

    if stage >= 2:
        sm = ExitStack()
        XW = T_ALL + 4
        xmT = sbx(sm, "xmT", [128, 4, XW], BF16)
        xcT = sbx(sm, "xcT", [128, 4, T_ALL], BF16)
        hbT = sbx(sm, "hbT", [128, 4, T_LAT], BF16)
        wbd = [sbx(sm, "wbd%d" % i, [128, 4, 128], BF16) for i in range(4)]
        hcb = sbx(sm, "hcb", [128, 4])
        tmp_th = sbx(sm, "tmp_th", [128, 512])
        tmp_zz = sbx(sm, "tmp_zz", [128, 512])
        lfhl = sbx(sm, "lfhl", [128, NT, 2, 2, 4], BF16)
        lihl = sbx(sm, "lihl", [128, NT, 2, 2, 4], BF16)
        tri_b = sbx(sm, "tri_b", [128, 2, 128], BF16)
        onesb128 = sbx(sm, "onesb128", [128, 128], BF16)
        sm2 = ExitStack()
        gts = sbx(sm2, "gts", [128, NT, 16])
        bgb = sbx(sm2, "bgb", [128, 16])
        wg = sbx(sm2, "wg", [128, 12, 16], BF16)
        lineg = sbx(sm2, "lineg", [128, NT, 2, 4])
        wsm = [sbx(sm2, "wsm%d" % i, [128, 4, 4]) for i in range(3)]
        BD = sbx(sm2, "BD", [128, 32, 4])
        dcw = sbx(sm2, "dcw", [128, 3, 4, 128], BF16)
        wxm = sbx(sm2, "wxm", [128, 8, 512], BF16)
        wxm32 = sbx(sm2, "wxm32", [128, 8, 512])
        wbdT = sbx(sm2, "wbdT", [128, 12, 128], BF16)
        Gc = sbx(sm2, "Gc", [128, 8, 16], BF16)
        lft = sbx(sm2, "lft", [128, NT, 2, 4])

        def xcol(tau):
            return tau + 1 if tau < 256 else tau + 3

        for kk in range(2):
            DMA("sp", wxm32[:, 4 * kk:4 * kk + 4, :], win_v[:, 4 * kk:4 * kk + 4, C_XM:C_XM + 512], w=[("wxm32", kk)])
            CP("act" if kk == 0 else "dve", wxm[:, 4 * kk:4 * kk + 4, :], wxm32[:, 4 * kk:4 * kk + 4, :],
               r=[("wxm32", kk)], w=[("wxm", kk)])
        MSET("dve", xmT[:, :, 0:1], 0.0, w=["xmT"])
        MSET("dve", xmT[:, :, 257:259], 0.0, w=["xmT"])
        MSET("dve", xmT[:, :, XW - 1:XW], 0.0, w=["xmT"])
        DMA("sp", bgb[:], bgate_d.partition_broadcast(128), w=["bgb"])
        DMA("pool", wg[:], wgate_d.rearrange("(c p) n -> p c n", p=128), w=["wg"])
        for i in range(3):
            DMA("sp", wsm[i][:], wm_d[i].rearrange("(h p) o -> p h o", p=128), w=[("wsm", i)])
        MSET("pool", BD[:], 1.0, w=["BD"])
        P.add("pool", lambda e: e.affine_select(out=BD[:], in_=BD[:], pattern=[[-4, 32], [0, 4]],
                                                compare_op=ALU.is_ge, fill=0.0, base=0, channel_multiplier=1),
              r=["BD"], w=["BD"])
        P.add("pool", lambda e: e.affine_select(out=BD[:], in_=BD[:], pattern=[[4, 32], [0, 4]],
                                                compare_op=ALU.is_ge, fill=0.0, base=3, channel_multiplier=-1),
              r=["BD"], w=["BD"])
        for i in range(3):
            for h in range(4):
                TT("dve", wbd[i][:, h, :].rearrange("p (g o) -> p g o", o=4), BD[:],
                   wsm[i][:, h, :].unsqueeze(1).to_broadcast([128, 32, 4]), ALU.mult,
                   r=["BD", ("wsm", i)], w=[("wbd", i, h)])
        TS("dve", wbd[3][:], wbd[1][:], 128.0 ** -0.5, None, ALU.mult, r=[("wbd", 1)], w=[("wbd", 3)])
        for j in range(3):
            for f in range(4):
                TS("dve", dcw[:, j, f, :], ident_f[:], vecs[:, V_CW + j * 4 + f:V_CW + j * 4 + f + 1], None, ALU.mult,
                   r=["ident_f", "vecs"], w=[("dcw", j, f)])
        TS("dve", hcb[:], vecs[:, V_CB:V_CB + 4], 0.5, None, ALU.mult, r=["vecs"], w=["hcb"])

        CP("pool", tri_b[:], tri_f[:], r=["tri_f"], w=["tri_b"])
        MSET("pool", onesb128[:], 1.0 / 128.0, w=["onesb128"])
        for (t0, N) in blocks:
            for f in range(4):
                pb = f % 2
                for k in range(8):
                    MM(psA[pb][:, 0:N], wxm[:, k, f * 128:(f + 1) * 128], uT[:, k, t0:t0 + N],
                       start=(k == 0), stop=(k == 7), r=["wxm", ("uT", k)], w=["psA%d" % pb])
                c0 = xcol(t0)
                CP("act" if pb == 0 else "dve", xmT[:, f, c0:c0 + N], psA[pb][:, 0:N],
                   r=["psA%d" % pb], w=[("xmT", f)])
        rot = [(psA[0], "psA0"), (psA[1], "psA1"), (psA[2], "psA2"), (psO, "psO"), (psC, "psC"), (psD, "psD"),
               (psE, "psE")]
        rot_i = [0]

        def nextbank():
            b_ = rot[rot_i[0] % len(rot)]
            rot_i[0] += 1
            return b_

        tmp2_th = sbx(sm2, "tmp2_th", [128, 512])
        tmp2_zz = sbx(sm2, "tmp2_zz", [128, 512])
        pT0 = psC[:].bitcast(BF16).rearrange("p (c t) -> p c t", t=128)
        pT1 = psD[:].bitcast(BF16).rearrange("p (c t) -> p c t", t=128)
        for i in range(3):
            for h in range(4):
                idx = i * 4 + h
                tgt, tk = (pT0, "psC") if idx < 8 else (pT1, "psD")
                P.add("pe", lambda e, i=i, h=h, idx=idx, tgt=tgt: e.transpose(out=tgt[:, idx % 8, :],
                                                                            in_=wbd[i][:, h, :],
                                                                            identity=ident_b[:]),
                      r=[("wbd", i, h), "ident_b"], w=[tk])
        CP("act", wbdT[:, 0:8, :], pT0[:, 0:8, :], r=["psC"], w=[("wbdT", 0)])
        CP("dve", wbdT[:, 8:12, :], pT1[:, 0:4, :], r=["psD"], w=[("wbdT", 1)])
        for h in range(4):
            MM(psB[:, h * 16:(h + 1) * 16], wbdT[:, h, :], wg[:, h, :], start=True, stop=False,
               r=[("wbdT", 0), "wg"], w=["psB"])
            MM(psB[:, h * 16:(h + 1) * 16], wbdT[:, 4 + h, :], wg[:, 4 + h, :], start=False, stop=True,
               r=[("wbdT", 0), "wg"], w=["psB"])
            MM(psB[:, 64 + h * 16:64 + (h + 1) * 16], wbdT[:, 8 + h, :], wg[:, 8 + h, :],
               r=[("wbdT", 1), "wg"], w=["psB"])
        CP("act", Gc[:].rearrange("p a n -> p (a n)"), psB[:, 0:128], r=["psB"], w=["Gc"])
        for (t0, N) in blocks:
            c0 = xcol(t0)
            for f in range(4):
                pt_, pk_ = nextbank()
                for j in range(3):
                    MM(pt_[:, 0:N], dcw[:, j, f, :], xmT[:, f, c0 + j - 1:c0 + j - 1 + N],
                       start=(j == 0), stop=(j == 2), r=[("dcw", j, f), ("xmT", f)], w=[pk_])
                if f % 2 == 0:
                    silu_from(pt_[:, 0:N], pk_, xcT[:, f, t0:t0 + N], ("xcT", f), None,
                              tmp_th[:, 0:N], tmp_zz[:, 0:N], bias_full=vecs[:, V_CB + f:V_CB + f + 1],
                              bias_half=hcb[:, f:f + 1], zz_on_act=True)
                else:
                    silu_from(pt_[:, 0:N], pk_, xcT[:, f, t0:t0 + N], ("xcT", f), None,
                              tmp2_th[:, 0:N], tmp2_zz[:, 0:N], bias_full=vecs[:, V_CB + f:V_CB + f + 1],
                              bias_half=hcb[:, f:f + 1], kth="tmp2_th", kzz="tmp2_zz", zz_on_act=True)
            for tt in range(N // 128):
                ti = (t0 + tt * 128) // 128
                pg_, pgk_ = nextbank()
                for h in range(4):
                    MM(pg_[:, 0:16], xcT[:, h, t0 + tt * 128:t0 + (tt + 1) * 128], Gc[:, h, :], start=(h == 0),
                       stop=False, r=[("xcT", h), "Gc"], w=[pgk_])
                for h in range(4):
                    MM(pg_[:, 0:16], xmT[:, h, c0 + tt * 128:c0 + (tt + 1) * 128], Gc[:, 4 + h, :], start=False,
                       stop=(h == 3), r=[("xmT", h), "Gc"], w=[pgk_])
                TT("dve", gts[:, ti, :], pg_[:, 0:16], bgb[:], ALU.add, r=[pgk_, "bgb"], w=[("gts", ti)])
        gts4 = gts[:].rearrange("p t (a b) -> p t a b", b=4)
        for dd in range(2):
            ACT(lft[:, :, dd, :], gts4[:, :, 2 * dd + 1, :], AF.Exp, r=["gts"], w=[("lft", dd)], scale=-1.0)
            ACT(lft[:, :, dd, :], lft[:, :, dd, :], AF.Ln, r=[("lft", dd)], w=[("lft", dd)], bias=1.0)
            TS("dve", gts4[:, :, 2 * dd + 1, :], lft[:, :, dd, :], -1.0, None, ALU.mult, r=[("lft", dd)], w=["gts"])

        for dd in range(2):
            TS("dve", lineg[:, :, dd, :], gts4[:, :, 2 * dd, :], -1.0, None, ALU.mult, r=["gts"], w=[("lineg", dd)])
            CP("dve", lihl[:, :, dd, 0, :], lineg[:, :, dd, :], r=[("lineg", dd)], w=[("lihl", dd, 0)])
            TT("dve", lihl[:, :, dd, 1, :], lineg[:, :, dd, :], lihl[:, :, dd, 0, :], ALU.subtract,
               r=[("lineg", dd), ("lihl", dd, 0)], w=[("lihl", dd, 1)])
            CP("dve", lfhl[:, :, dd, 0, :], gts4[:, :, 2 * dd + 1, :], r=["gts"], w=[("lfhl", dd, 0)])
            TT("dve", lfhl[:, :, dd, 1, :], gts4[:, :, 2 * dd + 1, :], lfhl[:, :, dd, 0, :], ALU.subtract,
               r=["gts", ("lfhl", dd, 0)], w=[("lfhl", dd, 1)])
        P.barrier()
        sm2.close()
        state32 = sbx(sm, "state32", [128, 4, 130])
        statebf = sbx(sm, "statebf", [128, 4, 130], BF16)
        Ebc = [sbx(sm, "Ebc%d" % i, [128, 4, 128]) for i in range(2)]
        wcol = [sbx(sm, "wcol%d" % i, [128, 4]) for i in range(2)]
        Kt = [sbx(sm, "Kt%d" % i, [128, 4, 128], BF16) for i in range(2)]
        Vp = [sbx(sm, "Vp%d" % i, [128, 4, 130], BF16) for i in range(2)]
        Qd = [sbx(sm, "Qd%d" % i, [128, 4, 128], BF16) for i in range(2)]
        KT = [sbx(sm, "KT%d" % i, [128, 4, 128], BF16) for i in range(2)]
        S0m = [sbx(sm, "S0m%d" % i, [128, 4, 128], BF16) for i in range(2)]
        dU = [sbx(sm, "dU%d" % i, [128, 4, 130]) for i in range(2)]
        absd = sbx(sm, "absd", [128, 4, 128])
        cellb = [sbx(sm, "cell%d" % i, [128, 4, 128]) for i in range(2)]
        hb1b = sbx(sm, "hb1b", [128, 4, 128], BF16)
        hb1q = sbx(sm, "hb1q", [128, 4, 128], BF16)
        mm_ = tmp_zz[:].rearrange("p (h t) -> p h t", t=128)
        msq = sbx(sm, "msq", [128, 4, 128])
        hb1q_f = sbx(sm, "hb1q_f", [128, 4, 128])
        soT = [sbx(sm, "soT%d" % i, [128, 4, 512], BF16) for i in range(2)]
        szT = [sbx(sm, "szT%d" % i, [128, 4, 512], BF16) for i in range(2)]
        tmpB_th = sbx(sm, "tmpB_th", [128, 512])
        tmpB_zz = sbx(sm, "tmpB_zz", [128, 512])
        ps4 = lambda t: t[:].rearrange("p (h t) -> p h t", t=128)

        def prep(dd, ti, p):
            lat = ti >= 2
            cs = slice(ti * 128, (ti + 1) * 128)
            xs = slice(xcol(ti * 128), xcol(ti * 128) + 128)
            tcol = 127 if dd == 0 else 0
            for h in range(4):
                for a in range(2):
                    MM(psB[:, h * 128:(h + 1) * 128], lfhl[:, ti, dd, a, h:h + 1].to_broadcast([128, 128]),
                       tri_b[:, dd, :], start=(a == 0), stop=(a == 1), r=["tri_b", ("lfhl", dd)], w=["psB"])
            for a in range(2):
                MM(psC[:, 0:4], tri_b[:, dd, :], lfhl[:, ti, dd, a, :], start=(a == 0), stop=False,
                   r=["tri_b", ("lfhl", dd)], w=["psC"])
            for a in range(2):
                MM(psC[:, 0:4], ident_b[:], lihl[:, ti, dd, a, :], start=False, stop=(a == 1),
                   r=["ident_b", ("lihl", dd)], w=["psC"])
            for h in range(4):
                MM(ps4(psD)[:, h, :], xcT[:, h, cs], wbd[3][:, h, :], r=[("xcT", h), ("wbd", 3)], w=["psD"])
            for h in range(4):
                MM(ps4(psE)[:, h, :], xmT[:, h, xs], wbd[2][:, h, :], r=[("xmT", h), ("wbd", 2)], w=["psE"])
            yield
            ACT(wcol[p][:], psC[:, 0:4], AF.Exp, r=["psC"], w=[("wcol", p)], scale=-1.0)
            ACT(Ebc[p][:].rearrange("p h t -> p (h t)"), psB[:], AF.Exp, r=["psB"], w=[("Ebc", p)])
            ACT(Kt[p][:].rearrange("p h t -> p (h t)"), psD[:], AF.Copy, r=["psD"], w=[("Kt", p)])
            yield
            TT("dve", Vp[p][:, :, 0:128], ps4(psE), wcol[p][:].unsqueeze(2).to_broadcast([128, 4, 128]), ALU.mult,
               r=["psE", ("wcol", p)], w=[("Vp", p, 0)])
            CP("act", Vp[p][:, :, 128:129], wcol[p][:].unsqueeze(2), r=[("wcol", p)], w=[("Vp", p, 1)])
            if lat:
                for h in range(4):
                    MM(ps4(psC)[:, h, :], wbd[0][:, h, :], xcT[:, h, cs], r=[("xcT", h), ("wbd", 0)], w=["psC"])
                for h in range(4):
                    MM(ps4(psB)[:, h, :], wbd[3][:, h, :], xcT[:, h, cs], r=[("xcT", h), ("wbd", 3)], w=["psB"])
            yield
            if lat:
                TT("dve", Qd[p][:].rearrange("p h t -> p (h t)"), psC[:], Ebc[p][:].rearrange("p h t -> p (h t)"),
                   ALU.mult, r=["psC", ("Ebc", p)], w=[("Qd", p)])
                CP("dve" if dd == 1 else "act", KT[p][:].rearrange("p h t -> p (h t)"), psB[:], r=["psB"],
                   w=[("KT", p)])
            for h in range(4):
                pu = (psD if h < 2 else psE)[:, (h % 2) * 256:(h % 2) * 256 + 129]
                MM(pu, Kt[p][:, h, :], Vp[p][:, h, 0:129], r=[("Kt", p), ("Vp", p)], w=["psD" if h < 2 else "psE"])
            yield
            if lat:
                for h in range(4):
                    MM(ps4(psC)[:, h, :], KT[p][:, h, :], Qd[p][:, h, :], r=[("KT", p), ("Qd", p)], w=["psC"])
            for h in range(4):
                pu = (psD if h < 2 else psE)[:, (h % 2) * 256:(h % 2) * 256 + 129]
                ACT(dU[p][:, h, 0:129], pu, AF.Identity, r=["psD" if h < 2 else "psE", ("Ebc", p)],
                    w=[("dU", p, h)], scale=Ebc[p][:, h, tcol:tcol + 1])
            yield
            if lat:
                TT("dve", S0m[p][:], ps4(psC), tri_b[:, dd, :].unsqueeze(1).to_broadcast([128, 4, 128]), ALU.mult,
                   r=["psC", "tri_b"], w=[("S0m", p)])
                yield

        def finish(dd, ti, p):
            lat = ti >= 2
            cs = slice(ti * 128, (ti + 1) * 128)
            tcol = 127 if dd == 0 else 0
            if lat:
                lc = slice((ti - 2) * 128, (ti - 1) * 128)
                for h in range(4):
                    MM(ps4(psA[2])[:, h, :], Vp[p][:, h, 0:128], S0m[p][:, h, :], start=True, stop=False,
                       r=[("Vp", p), ("S0m", p)], w=["psA2"])
                    MM(ps4(psA[2])[:, h, :], statebf[:, h, 0:128], Qd[p][:, h, :], start=False, stop=True,
                       r=["statebf", ("Qd", p)], w=["psA2"])
                for h in range(4):
                    MM(ps4(psO)[:, h, :], Vp[p][:, h, 128:129].to_broadcast([128, 128]), S0m[p][:, h, :],
                       start=True, stop=False, r=[("Vp", p), ("S0m", p)], w=["psO"])
                    MM(ps4(psO)[:, h, :], statebf[:, h, 128:129].to_broadcast([128, 128]), Qd[p][:, h, :],
                       start=False, stop=True, r=["statebf", ("Qd", p)], w=["psO"])
            for h in range(4):
                STT("dve", state32[:, h, 0:129], state32[:, h, 0:129], Ebc[p][:, h, tcol:tcol + 1], dU[p][:, h, 0:129],
                    ALU.mult, ALU.add, r=[("state32", h), ("Ebc", p), ("dU", p, h)], w=[("state32", h)])
            yield
            if lat:
                ACT(absd[:].rearrange("p h t -> p (h t)"), psO[:], AF.Abs, r=["psO"], w=["absd"])
            CP("act", statebf[:], state32[:], r=["state32"], w=["statebf"])
            yield
            if lat:
                TS("dve", absd[:], absd[:], 1.0, None, ALU.max, r=["absd"], w=["absd"])
                yield
                RECIP(absd[:], absd[:], r=["absd"], w=["absd"])
                yield
                if dd == 1:
                    TT("dve", hbT[:, :, lc], ps4(psA[2]), absd[:], ALU.mult, r=["psA2", "absd"], w=[("hbT", ti)])
                else:
                    TT("dve", cellb[p][:], ps4(psA[2]), absd[:], ALU.mult, r=["psA2", "absd"], w=[("cell", p)])
                yield

        def blockproj(b):
            pb = b % 2
            b0 = 256 + b * 512
            for (wt, dst, sig) in ((wom, soT[pb], True), (wzm, szT[pb], False)):
                for f in range(4):
                    pk = "psA0" if f % 2 == 0 else "psA1"
                    pt_ = psA[0] if f % 2 == 0 else psA[1]
                    for k in range(8):
                        MM(pt_[:], wt[:, k, f * 128:(f + 1) * 128], uT[:, k, b0:b0 + 512],
                           start=(k == 0), stop=(k == 7), r=["wom", "wzm", ("uT", k)], w=[pk])
                    if sig:
                        ACT(tmpB_th[:], pt_[:], AF.Tanh, r=[pk], w=["tmpB_th"], scale=0.5)
                        TS("dve", dst[:, f, :], tmpB_th[:], 0.5, 0.5, ALU.mult, ALU.add,
                           r=["tmpB_th"], w=[("soT", pb, f)])
                    else:
                        ACT(tmpB_th[:], pt_[:], AF.Tanh, r=[pk], w=["tmpB_th"], scale=0.5)
                        ACT(tmpB_zz[:], pt_[:], AF.Copy, r=[pk], w=["tmpB_zz"], scale=0.5)
                        STT("dve", dst[:, f, :], tmpB_th[:], 1.0, tmpB_zz[:], ALU.add, ALU.mult,
                            r=["tmpB_th", "tmpB_zz"], w=[("szT", pb, f)])
                    yield

        def epi(dd, ti, p):
            cell = cellb[p]
            cs = slice(ti * 128, (ti + 1) * 128)
            lc = slice((ti - 2) * 128, (ti - 1) * 128)
            lt = ti - 2
            pb = (lt // 4) % 2
            bc_ = slice((lt % 4) * 128, (lt % 4) * 128 + 128)
            TT("dve", cell[:], cell[:], hbT[:, :, lc], ALU.add, r=[("cell", p), ("hbT", ti)], w=[("cell", p)])
            TT("dve", cell[:], cell[:], soT[pb][:, :, bc_], ALU.mult, r=[("cell", p), ("soT", pb)], w=[("cell", p)])
            CP("act", hb1b[:], cell[:], r=[("cell", p)], w=["hb1b"])
            TT("dve", hb1q[:], cell[:], cell[:], ALU.mult, r=[("cell", p)], w=["hb1q"])
            for h in range(4):
                ACT(tmp_th[:, h * 128:(h + 1) * 128], xcT[:, h, cs], AF.Identity, r=[("xcT", h), "vecs"],
                    w=["tmp_th"], scale=vecs[:, V_SKIP + h:V_SKIP + h + 1])
            yield
            MM(psA[0][:], onesb128[:], hb1b[:].rearrange("p h t -> p (h t)"), r=["onesb128", "hb1b"], w=["psA0"])
            MM(psA[1][:], onesb128[:], hb1q[:].rearrange("p h t -> p (h t)"), r=["onesb128", "hb1q"], w=["psA1"])
            CP("act", mm_[:].rearrange("p h t -> p (h t)"), psA[0][:], r=["psA0"], w=["tmp_zz"])
            CP("act", msq[:].rearrange("p h t -> p (h t)"), psA[1][:], r=["psA1"], w=["msq"])
            ACT(hb1q_f[:], mm_[:], AF.Square, r=["tmp_zz"], w=["hb1q_f"])
            yield
            TT("dve", msq[:], msq[:], hb1q_f[:], ALU.subtract, r=["msq", "hb1q_f"], w=["msq"])
            TT("dve", cell[:], cell[:], mm_[:], ALU.subtract, r=[("cell", p), "tmp_zz"], w=[("cell", p)])
            yield
            RSQRT(msq[:], msq[:], LN_EPS, r=["msq"], w=["msq"])
            yield
            TT("dve", msq[:], msq[:], vecs[:, V_MHG:V_MHG + 4].unsqueeze(2).to_broadcast([128, 4, 128]), ALU.mult,
               r=["msq", "vecs"], w=["msq"])
            TT("dve", cell[:], cell[:], msq[:], ALU.mult, r=[("cell", p), "msq"], w=[("cell", p)])
            TT("dve", cell[:], cell[:], tmp_th[:].rearrange("p (h t) -> p h t", t=128), ALU.add,
               r=[("cell", p), "tmp_th"], w=[("cell", p)])
            TT("dve", ycatT[:, 4:8, lc], cell[:], szT[pb][:, :, bc_], ALU.mult, r=[("cell", p), ("szT", pb)],
               w=[("ycatT", 1, lt)])
            yield

        cnt_ = 0
        for dd in (1, 0):
            order = ([0, 1] + list(range(2, NT))) if dd == 0 else ([1, 0] + list(range(NT - 1, 1, -1)))
            MSET("pool", state32[:], 0.0, w=["state32"])
            MSET("pool", statebf[:], 0.0, w=["statebf"])
            par = {}
            for idx, ti in enumerate(order):
                par[ti] = cnt_ % 2
                cnt_ += 1
            for _ in prep(dd, order[0], par[order[0]]):
                pass
            def merge(gens):
                while gens:
                    for g in list(gens):
                        try:
                            next(g)
                        except StopIteration:
                            gens.remove(g)

            for idx, ti in enumerate(order):
                gens = []
                if dd == 0 and idx >= 1 and order[idx - 1] >= 2:
                    gens.append(epi(dd, order[idx - 1], par[order[idx - 1]]))
                gens.append(finish(dd, ti, par[ti]))
                if idx + 1 < len(order):
                    gens.append(prep(dd, order[idx + 1], par[order[idx + 1]]))
                if dd == 0 and idx == 0:
                    gens.append(blockproj(0))
                if dd == 0 and ti >= 2 and (ti - 2) % 4 == 1 and (ti - 2) // 4 + 1 < 4:
                    gens.append(blockproj((ti - 2) // 4 + 1))
                merge(gens)
            if dd == 0:
                merge([epi(dd, order[-1], par[order[-1]])])
        P.barrier()
        sm.close()
        sw.close()

    if stage >= 3:
        swo = ExitStack()
        wout = sbx(swo, "wout", [128, 8, D], BF16)
        sa = ExitStack()
        wqA = sbx(sa, "wqA", [128, 2, 8, 96], BF16)
        wqB = sbx(sa, "wqB", [128, 2, 8, 96], BF16)
        wkv = sbx(sa, "wkv", [128, 8, 128], BF16)
        rope = sbx(sa, "rope_sb", [96, 2, T_LAT])
        qnT = sbx(sa, "qnT", [128, 2, T_LAT], BF16)
        kvnT = sbx(sa, "kvnT", [128, T_ALL], BF16)
        szaT = sbx(sa, "szaT", [128, 4, T_LAT], BF16)
        tmp_th = sbx(sa, "tmp_th2", [128, 512])
        tmp_zz = sbx(sa, "tmp_zz2", [128, 512])
        onesq = sbx(sa, "onesq", [128, 128], BF16)
        onesk = sbx(sa, "onesk", [128, 128], BF16)
        Kh = [sbx(sa, "Kh%d" % i, [96, T_ALL], BF16) for i in range(2)]
        sa2 = ExitStack()
        watt = sbx(sa2, "watt", [128, 8, 960], BF16)
        raw = sbx(sa2, "raw", [128, 3, 512])
        sq = sbx(sa2, "sq", [128, 3, 512], BF16)
        rst = sbx(sa2, "rst", [128, 512])
        win32 = sbx(sa2, "win32", [128, 4, 928])
        wqb32 = sbx(sa2, "wqb32", [128, 2, 768])
        wkvb32 = sbx(sa2, "wkvb32", [128, 1024])
        MSET("dve", onesq[:], 1.0 / 256.0, w=["onesq"])
        MSET("dve", onesk[:], 1.0 / 128.0, w=["onesk"])
        MSET("dve", wqB[:], 0.0, w=["wqB"])
        for kk in range(2):
            ks = slice(4 * kk, 4 * kk + 4)
            DMA("sp", win32[:], win_v[:, ks, 0:928], w=["win32"])
            CP("act", watt[:, ks, 0:416], win32[:, :, 0:416], r=["win32"], w=[("watt", kk, 0)])
            for (dst, srcc) in ((0, 8), (8, 0), (16, 24), (24, 16)):
                CP("dve", watt[:, ks, 416 + dst:416 + dst + 8], win32[:, :, C_KR + srcc:C_KR + srcc + 8],
                   r=["win32"], w=[("watt", kk, 1)])
            CP("dve", watt[:, ks, 448:960], win32[:, :, C_ZA:C_ZA + 512], r=["win32"], w=[("watt", kk, 2)])
            if kk == 0:
                DMA("sp", wkvb32[:], wkvb_d, w=["wkvb32"])
                DMA("sp", wqb32[:], wqb_d.rearrange("(c p) n -> p c n", p=128), w=["wqb32"])
                for a in range(2):
                    DMA("sp", rope[64:96, a, :], rope_d[a], w=["rope"])
        wqb32v = wqb32[:].rearrange("p c (h n) -> p c h n", n=96)
        for c in range(2):
            CP("act", wqA[:, c, :, :], wqb32v[:, c, :, :], r=["wqb32"], w=["wqA"])
            for (dst, srcc) in ((0, 8), (8, 0), (16, 24), (24, 16)):
                CP("dve", wqB[:, c, :, 64 + dst:64 + dst + 8], wqb32v[:, c, :, 64 + srcc:64 + srcc + 8],
                   r=["wqb32"], w=["wqB"])
        CP("act", wkv[:], wkvb32[:].rearrange("p (h n) -> p h n", n=128), r=["wkvb32"], w=["wkv"])
        rotA = [(psA[0], "psA0"), (psA[1], "psA1"), (psA[2], "psA2"), (psO, "psO"), (psC, "psC"), (psD, "psD"),
                (psE, "psE")]
        rotA_i = [0]

        def nextbankA():
            b_ = rotA[rotA_i[0] % len(rotA)]
            rotA_i[0] += 1
            return b_

        rst2 = sbx(sa2, "rst2", [128, 512])
        for bi, (t0, N) in enumerate(blocks):
            lat = bi > 0
            l0 = t0 - 256
            for c in ((2, 0, 1) if lat else (2,)):
                pt_, pk_ = nextbankA()
                for k in range(8):
                    MM(pt_[:, 0:N], watt[:, k, c * 128:(c + 1) * 128], uT[:, k, t0:t0 + N],
                       start=(k == 0), stop=(k == 7), r=["watt", ("uT", k)], w=[pk_])
                CP("act", raw[:, c, 0:N], pt_[:, 0:N], r=[pk_], w=[("raw", c)])
                TT("dve", sq[:, c, 0:N], raw[:, c, 0:N], raw[:, c, 0:N], ALU.mult, r=[("raw", c)], w=[("sq", c)])
            ptA, pkA = nextbankA()
            for k in range(8):
                MM(ptA[0:96, 0:N], watt[:, k, 320:416], uT[:, k, t0:t0 + N], start=(k == 0), stop=(k == 7),
                   r=["watt", ("uT", k)], w=[pkA])
            if lat:
                ptB, pkB = nextbankA()
                for k in range(8):
                    MM(ptB[0:96, 0:N], watt[:, k, 352:448], uT[:, k, t0:t0 + N], start=(k == 0), stop=(k == 7),
                       r=["watt", ("uT", k)], w=[pkB])
            pt_, pk_ = nextbankA()
            MM(pt_[:, 0:N], onesk[:], sq[:, 2, 0:N], r=["onesk", ("sq", 2)], w=[pk_])
            RSQRT(rst2[:, 0:N], pt_[:, 0:N], RMS_EPS, r=[pk_], w=["rst2"])
            STT("dve", kvnT[:, t0:t0 + N], raw[:, 2, 0:N], vecs[:, V_GKVA:V_GKVA + 1], rst2[:, 0:N],
                ALU.mult, ALU.mult, r=[("raw", 2), "vecs", "rst2"], w=["kvnT"])
            if lat:
                for c in range(2):
                    MM(psB[:, 0:N], onesq[:], sq[:, c, 0:N], start=(c == 0), stop=(c == 1), r=["onesq", ("sq", c)],
                       w=["psB"])
                RSQRT(rst[:, 0:N], psB[:, 0:N], RMS_EPS, r=["psB"], w=["rst"])
                for c in range(2):
                    STT("dve", qnT[:, c, l0:l0 + N], raw[:, c, 0:N], vecs[:, V_GQA + c:V_GQA + c + 1], rst[:, 0:N],
                        ALU.mult, ALU.mult, r=[("raw", c), "vecs", "rst"], w=[("qnT", c)])
                TT("dve", tmp_th[64:96, 0:N], ptA[64:96, 0:N], rope[64:96, 0, l0:l0 + N], ALU.mult,
                   r=[pkA, "rope"], w=["tmp_th"])
                TT("dve", tmp_zz[64:96, 0:N], ptB[64:96, 0:N], rope[64:96, 1, l0:l0 + N], ALU.mult,
                   r=[pkB, "rope"], w=["tmp_zz"])
                TT("dve", Kh[0][64:96, t0:t0 + N], tmp_th[64:96, 0:N], tmp_zz[64:96, 0:N], ALU.add,
                   r=["tmp_th", "tmp_zz"], w=[("Kh", 0, "r")])
                CP("act", Kh[1][64:96, t0:t0 + N], Kh[0][64:96, t0:t0 + N], r=[("Kh", 0, "r")], w=[("Kh", 1, "r")])
                for f in range(4):
                    pt_, pk_ = nextbankA()
                    for k in range(8):
                        MM(pt_[:, 0:N], watt[:, k, 448 + f * 128:448 + (f + 1) * 128], uT[:, k, t0:t0 + N],
                           start=(k == 0), stop=(k == 7), r=["watt", ("uT", k)], w=[pk_])
                    if f % 2 == 0:
                        silu_from(pt_[:, 0:N], pk_, szaT[:, f, l0:l0 + N], ("szaT", f), None,
                                  tmp_th[:, 0:N], tmp_zz[:, 0:N])
                    else:
                        silu_from(pt_[:, 0:N], pk_, szaT[:, f, l0:l0 + N], ("szaT", f), None,
                                  rst2[:, 0:N], rst[:, 0:N], kth="rst2", kzz="rst")
            else:
                CP("act", Kh[0][64:96, t0:t0 + N], ptA[64:96, 0:N], r=[pkA], w=[("Kh", 0, "r")])
                CP("dve", Kh[1][64:96, t0:t0 + N], ptA[64:96, 0:N], r=[pkA], w=[("Kh", 1, "r")])

        P.barrier()
        sa2.close()
        Qh = [sbx(sa, "Qh%d" % i, [96, T_LAT], BF16) for i in range(2)]
        Vh = [sbx(sa, "Vh%d" % i, [128, NT, 128], BF16) for i in range(2)]
        PT = [sbx(sa, "PT%d" % i, [128, 512], BF16) for i in range(4)]
        rden = [sbx(sa, "rden%d" % i, [128, 512]) for i in range(2)]
        gg = [sbx(sa, "gg%d" % i, [128, 512]) for i in range(2)]
        rhl = [sbx(sa, "rhl%d" % i, [128, 2, 512], BF16) for i in range(2)]
        DMA("pool", wout[:], wout_d.rearrange("(c p) n -> p c n", p=128), w=["wout"])
        MSET("dve", Vh[0][:], 0.0, w=[("Vh", 0)])
        MSET("dve", Vh[1][:], 0.0, w=[("Vh", 1)])
        MSET("dve", Vh[0][:, :, 64:65], 1.0, w=[("Vh", 0)])
        MSET("dve", Vh[1][:, :, 0:1], 1.0, w=[("Vh", 1)])
        def prologue(h):
            hb_ = h % 2
            voff = 0 if hb_ == 0 else 64
            for bi, (t0, N) in enumerate(blocks):
                MM(psC[0:64, 0:N], wkv[:, h, 0:64], kvnT[:, t0:t0 + N], r=["wkv", "kvnT"], w=["psC"])
                CP("dve", Kh[hb_][0:64, t0:t0 + N], psC[0:64, 0:N], r=["psC"], w=[("Kh", hb_, "n")])
                yield
            for g0 in range(0, NT, 8):
                nt_ = min(8, NT - g0)
                pv = psD[:].rearrange("p (j n) -> p j n", n=64)
                for jj in range(nt_):
                    MM(pv[:, jj, :], kvnT[:, (g0 + jj) * 128:(g0 + jj + 1) * 128], wkv[:, h, 64:128],
                       r=["wkv", "kvnT"], w=["psD"])
                CP("dve", Vh[hb_][:, g0:g0 + nt_, voff:voff + 64], pv[:, 0:nt_, :], r=["psD"], w=[("Vh", hb_)])
                yield
            for qb in range(4):
                qs = slice(qb * 512, (qb + 1) * 512)
                for c in range(2):
                    MM(psC[0:96, :], wqA[:, c, h, :], qnT[:, c, qs], start=(c == 0), stop=(c == 1),
                       r=["wqA", ("qnT", c)], w=["psC"])
                    yield
                for c in range(2):
                    MM(psD[0:96, :], wqB[:, c, h, :], qnT[:, c, qs], start=(c == 0), stop=(c == 1),
                       r=["wqB", ("qnT", c)], w=["psD"])
                    if c == 0:
                        yield
                CP("dve", Qh[hb_][0:64, qs], psC[0:64, :], r=["psC"], w=[("Qh", hb_, qb)])
                TT("dve", tmp_th[64:96, :], psC[64:96, :], rope[64:96, 0, qs], ALU.mult, r=["psC", "rope"],
                   w=["tmp_th"])
                TT("dve", tmp_zz[64:96, :], psD[64:96, :], rope[64:96, 1, qs], ALU.mult, r=["psD", "rope"],
                   w=["tmp_zz"])
                TT("dve", Qh[hb_][64:96, qs], tmp_th[64:96, :], tmp_zz[64:96, :], ALU.add,
                   r=["tmp_th", "tmp_zz"], w=[("Qh", hb_, qb)])
                yield

        pending = []

        def mainloop(h, pro):
            hb_ = h % 2
            M = 65 if hb_ == 0 else 128
            pd = 64 if hb_ == 0 else 0
            r0 = 0 if hb_ == 0 else 64
            for qb in range(4):
                qs = slice(qb * 512, (qb + 1) * 512)
                ob = (h * 4 + qb) % 2
                pso = psO if ob == 0 else psE
                pok = "psO" if ob == 0 else "psE"
                SK = 3
                for step in range(NT + SK):
                    if step < NT:
                        sbk = step % 3
                        MM(psA[sbk][:], Kh[hb_][:, step * 128:(step + 1) * 128], Qh[hb_][:, qs],
                           r=[("Kh", hb_), ("Qh", hb_, qb)], w=["psA%d" % sbk])
                        ACT(PT[step % 4][:], psA[sbk][:], AF.Exp, r=["psA%d" % sbk], w=[("PT", step % 4)],
                            scale=ATT_SCALE)
                    if step >= SK:
                        kt = step - SK
                        MM(pso[0:M, :], Vh[hb_][:, kt, 0:M], PT[kt % 4][:], start=(kt == 0), stop=(kt == NT - 1),
                           r=[("Vh", hb_), ("PT", kt % 4)], w=[pok])
                    if pro is not None and step % 3 == 1:
                        next(pro, None)
                    if step == 3 and pending:
                        pending.pop(0)()
                    if step == 8 and pending:
                        pending.pop(0)()

                def ep1(ob=ob, pso=pso, pok=pok, pd=pd):
                    RECIP(rden[ob][pd:pd + 1, :], pso[pd:pd + 1, :], r=[pok], w=[("rden", ob)])
                    CP("dve", rhl[ob][pd:pd + 1, 0, :], rden[ob][pd:pd + 1, :], r=[("rden", ob)], w=[("rhl", ob)])
                    TT("dve", rhl[ob][pd:pd + 1, 1, :], rden[ob][pd:pd + 1, :], rhl[ob][pd:pd + 1, 0, :], ALU.subtract,
                       r=[("rden", ob), ("rhl", ob)], w=[("rhl", ob)])

                def ep2(ob=ob, pso=pso, pok=pok, pd=pd, r0=r0, qs=qs, h=h, qb=qb):
                    MM(psB[:], ones_b[pd:pd + 1, :], rhl[ob][pd:pd + 1, 0, :], start=True, stop=False,
                       r=["ones_b", ("rhl", ob)], w=["psB"])
                    MM(psB[:], ones_b[pd:pd + 1, :], rhl[ob][pd:pd + 1, 1, :], start=False, stop=True,
                       r=["ones_b", ("rhl", ob)], w=["psB"])
                    TT("dve", gg[ob][r0:r0 + 64, :], psB[r0:r0 + 64, :], szaT[r0:r0 + 64, h // 2, qs], ALU.mult,
                       r=["psB", ("szaT", h // 2)], w=[("gg", ob)])
                    TT("dve", ycatT[r0:r0 + 64, h // 2, qs], pso[r0:r0 + 64, :], gg[ob][r0:r0 + 64, :], ALU.mult,
                       r=[pok, ("gg", ob)], w=[("ycatT", 0, h, qb)])
                pending.append(ep1)
                pending.append(ep2)

        for _ in prologue(0):
            pass
        for h in range(8):
            pro = prologue(h + 1) if h < 7 else None
            mainloop(h, pro)
            if pro is not None:
                for _ in pro:
                    pass
        while pending:
            pending.pop(0)()
        P.barrier()
        sa.close()

    if stage >= 4:
        so = ExitStack()
        gate_bc = sbx(so, "gate_bc", [128, D])
        dg = sbx(so, "dg", [128, 2, 128])
        for c in range(8):
            TS("dve", dg[:, c % 2, :], ident_f[:], ada[:, 16 + c, 0:1], None, ALU.mult, r=["ident_f", ("ada", 2, 0)],
               w=[("dg", c % 2)])
            MM(psA[c % 2][:, 0:128], ones_f[:], dg[:, c % 2, :], r=["ones_f", ("dg", c % 2)], w=["psA%d" % (c % 2)])
            CP("act", gate_bc[:, c * 128:(c + 1) * 128], psA[c % 2][:, 0:128], r=["psA%d" % (c % 2)], w=[("gate_bc", c)])
        rowsb = sbx(so, "rowsb", [128, 4, D])
        xin = [sbx(so, "xin%d" % i, [128, D]) for i in range(3)]
        NPRE = 2
        for a in range(2):
            DMA("sp", rowsb[:, a, :], rows_d[a:a + 1, :].partition_broadcast(128), w=[("rowsb", a)])
        for lt in range(NPRE):
            DMA("sp", xin[lt % 3][:], x_d[lt * 128:(lt + 1) * 128, :], w=[("xin", lt % 3)])
        for a in range(2, 4):
            DMA("sp", rowsb[:, a, :], rows_d[a:a + 1, :].partition_broadcast(128), w=[("rowsb", a)])
        for a in range(2):
            TS("dve", rowsb[:, a, :], rowsb[:, a, :], ALPHA, None, ALU.mult, r=[("rowsb", a)], w=[("rowsb", a)])
        for c in range(8):
            cs_ = slice(c * 128, (c + 1) * 128)
            TT("dve", wout[:, :, cs_], wout[:, :, cs_], gate_bc[:, cs_].unsqueeze(1).to_broadcast([128, 8, 128]),
               ALU.mult, r=["wout", ("gate_bc", c)], w=["wout"])
        bhl = sbx(so, "bhl", [1, 2, D], BF16)
        CP("dve", bhl[0:1, 0, :], rowsb[0:1, 1, :], r=[("rowsb", 1)], w=["bhl"])
        TT("dve", bhl[0:1, 1, :], rowsb[0:1, 1, :], bhl[0:1, 0, :], ALU.subtract, r=[("rowsb", 1), "bhl"], w=["bhl"])
        nmr = sbx(so, "nmr", [128, NT])
        STT("dve", nmr[:], stats[:, :, 0], -1.0, stats[:, :, 1], ALU.mult, ALU.mult, r=["stats"], w=["nmr"])
        nmo = sbx(so, "nmo", [128, 1])
        xo = [sbx(so, "xo%d" % i, [128, D]) for i in range(2)]
        pre = [sbx(so, "pre%d" % i, [128, D]) for i in range(2)]
        st6o = sbx(so, "st6o", [128, 2, 6])
        mvo = sbx(so, "mvo", [128, 2])

        def prefetch(lt):
            if lt + NPRE < 16:
                nb = (lt + NPRE) % 3
                DMA("sp", xin[nb][:], x_d[(lt + NPRE) * 128:(lt + NPRE + 1) * 128, :], w=[("xin", nb)])

        def stage1(lt):
            xb = lt % 3
            ti = lt + 2
            ACT(xin[xb][:], xin[xb][:], AF.Identity, r=[("xin", xb), "nmr"], w=[("xin", xb)],
                scale=stats[:, ti, 1:2], bias=nmr[:, ti:ti + 1])
            TT("dve", xin[xb][:], xin[xb][:], rowsb[:, 0, :], ALU.mult, r=[("xin", xb), ("rowsb", 0)],
               w=[("xin", xb)])

        prefetch(0)
        stage1(0)
        for lt in range(16):
            b = lt % 2
            xb = lt % 3
            cs = slice(lt * 128, (lt + 1) * 128)
            if lt + 1 < 16:
                stage1(lt + 1)
            bankset = [((psA[0], "psA0"), (psA[1], "psA1")), ((psA[2], "psA2"), (psO, "psO")),
                       ((psC, "psC"), (psD, "psD"))][lt % 3]
            for hf in range(2):
                hs = slice(hf * 512, (hf + 1) * 512)
                pt_, pk_ = bankset[hf]
                for k in range(8):
                    MM(pt_[:], ycatT[:, k, cs], wout[:, k, hs], start=(k == 0), stop=False,
                       r=["ycatT", ("wout", k)], w=[pk_])
                for a in range(2):
                    MM(pt_[:], ones_b[0:1, :], bhl[0:1, a, hs], start=False, stop=(a == 1),
                       r=["ones_b", "bhl"], w=[pk_])
            for hf in range(2):
                hs = slice(hf * 512, (hf + 1) * 512)
                pt_, pk_ = bankset[hf]
                TT("dve", pre[b][:, hs], pt_[:], xin[xb][:, hs], ALU.add, r=[pk_, ("xin", xb)],
                   w=[("pre", b, hf)])
                P.add("dve", lambda e, b=b, hf=hf, hs=hs: e.bn_stats(out=st6o[:, hf, :], in_=pre[b][:, hs]),
                      r=[("pre", b, hf)], w=[("st6o", hf)])
            P.add("dve", lambda e: e.bn_aggr(out=mvo[:], in_=st6o[:]), r=["st6o"], w=["mvo"])
            RSQRT(mvo[:, 1:2], mvo[:, 1:2], LN_EPS, r=["mvo"], w=["mvo"])
            STT("dve", nmo[:], mvo[:, 0:1], -1.0, mvo[:, 1:2], ALU.mult, ALU.mult, r=["mvo"], w=["nmo"])
            if lt + 1 < 16:
                prefetch(lt + 1)
            for hf in range(2):
                hs = slice(hf * 512, (hf + 1) * 512)
                ACT(pre[b][:, hs], pre[b][:, hs], AF.Identity, r=[("pre", b, hf), "mvo", "nmo"], w=[("pre", b, hf)],
                    scale=mvo[:, 1:2], bias=nmo[:])
            for hf in range(2):
                hs = slice(hf * 512, (hf + 1) * 512)
                TT("dve", pre[b][:, hs], pre[b][:, hs], rowsb[:, 2, hs], ALU.mult, r=[("pre", b, hf), ("rowsb", 2)],
                   w=[("pre", b, hf)])
                TT("dve", xo[b][:, hs], pre[b][:, hs], rowsb[:, 3, hs], ALU.add, r=[("pre", b, hf), ("rowsb", 3)],
                   w=[("xo", b, hf)])
            DMA("sp", out_d[cs, :], xo[b][:], r=[("xo", b)], w=[("outd", lt)])
        P.add("sp", lambda e: e.nop(), r=["outd"])
        P.barrier()
        so.close()
        swo.close()

    P.barrier()
    info = P.emit()
    es.close()
    return nc, info


def _rope_tables():
    n_rows = T_LAT // 64
    row = np.repeat(np.arange(n_rows, dtype=np.float32), 64)
    col = np.tile(np.arange(64, dtype=np.float32), n_rows)
    inv = (np.float32(10000.0) ** (-np.arange(8, dtype=np.float32) / np.float32(8))).astype(np.float32)
    ang = np.stack([row[:, None] * inv, col[:, None] * inv], axis=1).astype(np.float32)
    cos = np.cos(ang).astype(np.float32)
    sin = np.sin(ang).astype(np.float32)
    tab = np.zeros((2, 32, T_LAT), np.float32)
    for ax in range(2):
        for half in range(2):
            for f in range(8):
                r = ax * 16 + half * 8 + f
                tab[0, r] = cos[:, ax, f]
                tab[1, r] = -sin[:, ax, f] if half == 0 else sin[:, ax, f]
    return tab


def _fm(v):
    return np.asarray(v, np.float32).reshape(-1, 128).T


def make_in_maps(inp, ncores=8):
    rope = _rope_tables()
    rows = np.ascontiguousarray(np.stack([inp["ln_in_g"], inp["ln_in_b"], inp["ln_g"][0], inp["ln_b"][0]]).astype(np.float32))
    vecs = np.ascontiguousarray(np.concatenate(
        [_fm(inp["ln_in_g"]), _fm(inp["ln_in_b"]), _fm(inp["b_ada"][0]), _fm(inp["conv_w"][0]), _fm(inp["conv_b"][0]),
         _fm(inp["mh_g"][0]), _fm(inp["skip"][0]), _fm(inp["g_qa"][0]), _fm(inp["g_kva"][0])], axis=1).astype(np.float32))
    assert vecs.shape == (128, NV)
    maps = []
    for b in range(ncores):
        cc = np.stack([inp["c"][b], inp["c_ctx"]], axis=-1)
        ccT = np.ascontiguousarray(cc.reshape(8, 128, 2).transpose(1, 0, 2)).astype(np.float32)
        maps.append(dict(
            x=np.ascontiguousarray(inp["x"][b]), ctx=np.ascontiguousarray(inp["ctx"][b]), ccT=ccT, vecs=vecs,
            w_ada=np.ascontiguousarray(inp["w_ada"][0]), w_in=np.ascontiguousarray(inp["w_in"][0]),
            w_qb=np.ascontiguousarray(inp["w_qb"][0]), w_kvb=np.ascontiguousarray(inp["w_kvb"][0]),
            w_out=np.ascontiguousarray(inp["w_out"][0]), w_gate=np.ascontiguousarray(inp["w_gate"][0]),
            b_gate=np.ascontiguousarray(inp["b_gate"][0].reshape(1, 16)),
            w_mq=np.ascontiguousarray(inp["w_mq"][0].reshape(512, 4)),
            w_mk=np.ascontiguousarray(inp["w_mk"][0].reshape(512, 4)),
            w_mv=np.ascontiguousarray(inp["w_mv"][0].reshape(512, 4)),
            rows=rows, rope=rope))
    return maps


_NC_CACHE = {}


def kernel(**inputs):
    inp = {k: np.asarray(v) for k, v in inputs.items()}
    if "nc" not in _NC_CACHE:
        _NC_CACHE["nc"] = build_nc()[0]
    nc = _NC_CACHE["nc"]
    maps = make_in_maps(inp, 8)
    res = run_bass_kernel_spmd(nc, maps, core_ids=list(range(8)))
    out = np.stack([np.asarray(r["out"], dtype=np.float32) for r in res.results], axis=0)
    return out
```

```python
import numpy as np
from contextlib import ExitStack
import concourse.bass as bass
import concourse.mybir as mybir
from concourse.bass_utils import run_bass_kernel_spmd

F32 = mybir.dt.float32
BF16 = mybir.dt.bfloat16
AF = mybir.ActivationFunctionType
ALU = mybir.AluOpType

N_DMA_SEMS = 24

D = 1024
T_LAT = 2048
T_CTX = 256
T_ALL = T_LAT + T_CTX
NT = T_ALL // 128
LN_EPS = 1e-5
RMS_EPS = 1e-6
ALPHA = 2.0 ** 0.25
ATT_SCALE = 96.0 ** -0.5
NV = 67
V_GIN, V_BIN, V_BADA, V_CW, V_CB, V_MHG, V_SKIP, V_GQA, V_GKVA = 0, 8, 16, 40, 52, 56, 60, 64, 66
C_QA, C_KVA, C_KR, C_ZA, C_XM, C_OM, C_ZM = 0, 256, 384, 416, 928, 1440, 1952


class Prog:
    def __init__(self, nc, es):
        self.nc = nc
        self.es = es
        self.ops = []
        self.acc = {}
        self.dma_rr = {"pool": 0, "hw": 0}
        self.dma_rng = {"pool": (0, 8), "hw": (8, N_DMA_SEMS)}
        self.dma_last = [None] * N_DMA_SEMS
        self.dma_cnt = [0] * N_DMA_SEMS
        self.barrier_op = None
        self.last_eng = {}
        self.open_dma = []
        self.flushed = 0
        self.engs = ["pe", "act", "dve", "pool", "sp"]
        self.esem = {e: es.enter_context(nc.semaphore("s_" + e)) for e in self.engs}
        self.dsem = [es.enter_context(nc.semaphore("d_%d" % k)) for k in range(N_DMA_SEMS)]
        self.cnt = {e: 0 for e in self.engs}
        self.seen = {e: {} for e in self.engs}
        self.nwaits = 0

    @staticmethod
    def _conf(a, b):
        n = min(len(a), len(b))
        return a[:n] == b[:n]

    def add(self, eng, fn, r=(), w=(), dma=False, sticky=False):
        i = len(self.ops)
        deps = set()
        r = [k if isinstance(k, tuple) else (k,) for k in r]
        w = [k if isinstance(k, tuple) else (k,) for k in w]
        for k in list(r):
            if k[0].startswith("ps"):
                r.remove(k)
                w.append((k[0],))
        w = [(k[0],) if k[0].startswith("ps") else k for k in w]
        for k in r:
            d = self.acc.setdefault(k[0], {})
            for sk, st in d.items():
                if self._conf(sk, k[1:]) and st[0] is not None:
                    deps.add(st[0])
        for k in w:
            d = self.acc.setdefault(k[0], {})
            for sk, st in d.items():
                if self._conf(sk, k[1:]):
                    if st[0] is not None:
                        deps.add(st[0])
                    deps.update(st[1])
        for k in r:
            d = self.acc[k[0]]
            st = d.setdefault(k[1:], [None, []])
            st[1].append(i)
        for k in w:
            d = self.acc[k[0]]
            for sk in [sk for sk in d if len(sk) >= len(k[1:]) and self._conf(sk, k[1:])]:
                del d[sk]
            d[k[1:]] = [i, []]
        if self.barrier_op is not None:
            deps.add(self.barrier_op)
        deps.discard(i)
        self.last_eng[eng] = i
        if dma and not sticky:
            self.open_dma.append(i)
        op = dict(eng=eng, fn=fn, deps=deps, dma=dma, sig=False, sem=None, val=None, sticky=sticky)
        if dma:
            cls = "pool" if eng == "pool" else "hw"
            lo, hi = self.dma_rng[cls]
            s = lo + self.dma_rr[cls]
            self.dma_rr[cls] = (self.dma_rr[cls] + 1) % (hi - lo)
            if self.dma_last[s] is not None:
                op["deps"].add(self.dma_last[s])
            self.dma_last[s] = i
            self.dma_cnt[s] += 16
            op["dsem"] = s
            op["val"] = self.dma_cnt[s]
        self.ops.append(op)
        return i

    def barrier(self):
        deps = set(self.last_eng.values()) | set(self.open_dma)
        self.open_dma = []
        i = self.add("pool", lambda e: e.nop())
        self.ops[i]["deps"].update(d for d in deps if d != i)
        self.ops[i]["sig"] = True
        self.ops[i]["is_bar"] = True
        self.barrier_op = i
        self.flush()
        return i

    def flush(self):
        nc = self.nc
        ops = self.ops
        base = self.flushed
        engs = self.engs
        eobj = dict(pe=nc.tensor, act=nc.scalar, dve=nc.vector, pool=nc.gpsimd, sp=nc.sync)
        for i in range(base, len(ops)):
            op = ops[i]
            op["deps"] = {d for d in op["deps"] if d >= base or ops[d].get("is_bar") or ops[d].get("sticky")}
            for d in op["deps"]:
                dop = ops[d]
                if dop["dma"]:
                    continue
                if dop["eng"] == "pe" and op["eng"] == "pe" and not op["dma"]:
                    continue
                dop["sig"] = True
        for i in range(base, len(ops)):
            op = ops[i]
            if op["dma"]:
                op["sem"] = self.dsem[op["dsem"]]
            elif op["sig"]:
                self.cnt[op["eng"]] += 1
                op["sem"] = self.esem[op["eng"]]
                op["val"] = self.cnt[op["eng"]]
        per = {e: [] for e in engs}
        for i in range(base, len(ops)):
            per[ops[i]["eng"]].append(i)

        def run(e):
            eng = eobj[e]
            seen = self.seen[e]
            for i in per[e]:
                op = ops[i]
                need = {}
                for d in op["deps"]:
                    dop = ops[d]
                    if (not dop["dma"]) and dop["eng"] == "pe" and e == "pe" and not op["dma"]:
                        continue
                    sem = dop["sem"]
                    key = id(sem)
                    if dop["val"] > need.get(key, (None, 0))[1]:
                        need[key] = (sem, dop["val"])
                for key, (sem, val) in need.items():
                    if seen.get(key, 0) >= val:
                        continue
                    eng.wait_ge(sem, val)
                    self.nwaits += 1
                    seen[key] = val
                ins = op["fn"](eng)
                if op["dma"]:
                    ins.then_inc(op["sem"], 16)
                elif op["sig"]:
                    ins.then_inc(op["sem"], 1)
                op["fn"] = None

        with nc.Block() as block:
            @block.tensor
            def _(e):
                run("pe")

            @block.scalar
            def _(e):
                run("act")

            @block.vector
            def _(e):
                run("dve")

            @block.gpsimd
            def _(e):
                run("pool")

            @block.sync
            def _(e):
                run("sp")
        self.flushed = len(ops)

    def emit(self):
        if self.flushed < len(self.ops):
            self.barrier()
        return dict(n_ops=len(self.ops), n_waits=self.nwaits, sig=dict(self.cnt))


def build_nc(stage=99, dbg=None):
    dbg = dbg or {}
    nc = bass.Bass("TRN2", target_bir_lowering=False)
    es = ExitStack()
    P = Prog(nc, es)

    def dram(name, shape, dt=F32, out=False):
        return nc.dram_tensor(name, list(shape), dt, kind="ExternalOutput" if out else "ExternalInput").ap()

    x_d = dram("x", [T_LAT, D])
    ctx_d = dram("ctx", [T_CTX, D])
    cc_d = dram("ccT", [128, 8, 2])
    vecs_d = dram("vecs", [128, NV])
    wada_d = dram("w_ada", [D, 3 * D])
    win_d = dram("w_in", [D, 2464])
    wqb_d = dram("w_qb", [256, 768])
    wkvb_d = dram("w_kvb", [128, 1024])
    wout_d = dram("w_out", [D, D])
    wgate_d = dram("w_gate", [1536, 16])
    bgate_d = dram("b_gate", [1, 16])
    wm_d = [dram(n, [512, 4]) for n in ("w_mq", "w_mk", "w_mv")]
    rows_d = dram("rows", [4, D])
    rope_d = dram("rope", [2, 32, T_LAT])
    out_d = dram("out", [T_LAT, D], out=True)
    dbg_d = {k: dram(k, shp, out=True) for k, shp in dbg.items()}

    def sbx(stack, name, shape, dt=F32):
        return stack.enter_context(nc.sbuf_tensor(name, list(shape), dt))

    def sb(name, shape, dt=F32):
        return sbx(es, name, shape, dt)

    def MM(out, lhsT, rhs, start=True, stop=True, r=(), w=()):
        P.add("pe", lambda e: e.matmul(out, lhsT=lhsT, rhs=rhs, start=start, stop=stop), r=r, w=w)

    def ACT(out, in_, func, r=(), w=(), scale=1.0, bias=None):
        if bias is None:
            P.add("act", lambda e: e.activation(out=out, in_=in_, func=func, scale=scale), r=r, w=w)
        else:
            P.add("act", lambda e: e.activation(out=out, in_=in_, func=func, scale=scale, bias=bias), r=r, w=w)

    def TS(eng, out, in0, s1, s2, op0, op1=None, r=(), w=()):
        if op1 is None:
            P.add(eng, lambda e: e.tensor_scalar(out=out, in0=in0, scalar1=s1, scalar2=None, op0=op0), r=r, w=w)
        else:
            P.add(eng, lambda e: e.tensor_scalar(out=out, in0=in0, scalar1=s1, scalar2=s2, op0=op0, op1=op1), r=r, w=w)

    def TT(eng, out, in0, in1, op, r=(), w=()):
        P.add(eng, lambda e: e.tensor_tensor(out=out, in0=in0, in1=in1, op=op), r=r, w=w)

    def STT(eng, out, in0, scalar, in1, op0, op1, r=(), w=()):
        P.add(eng, lambda e: e.scalar_tensor_tensor(out=out, in0=in0, scalar=scalar, in1=in1, op0=op0, op1=op1),
              r=r, w=w)

    def CP(eng, out, in_, r=(), w=()):
        if eng == "act":
            P.add("act", lambda e: e.activation(out=out, in_=in_, func=AF.Copy), r=r, w=w)
        else:
            P.add(eng, lambda e: e.tensor_copy(out=out, in_=in_), r=r, w=w)

    def MSET(eng, ap, val, w=()):
        P.add(eng, lambda e: e.memset(ap, val), w=w)

    def RSQRT(out, in_, eps, r=(), w=()):
        ACT(out, in_, AF.Ln, r=r, w=w, bias=eps)
        ACT(out, out, AF.Exp, r=w, w=w, scale=-0.5)

    def RECIP(out, in_, r=(), w=()):
        ACT(out, in_, AF.Ln, r=r, w=w)
        ACT(out, out, AF.Exp, r=w, w=w, scale=-1.0)

    def DMA(q, out, in_, r=(), w=(), sticky=False):
        P.add(q, lambda e: e.dma_start(out=out, in_=in_), r=r, w=w, dma=True, sticky=sticky)

    def silu_from(ps_ap, ps_key, out_ap, out_key, shape, tmp_th, tmp_zz, bias_full=None, bias_half=None,
                  kth="tmp_th", kzz="tmp_zz", zz_on_act=False):
        if bias_half is None:
            ACT(tmp_th, ps_ap, AF.Tanh, r=[ps_key], w=[kth], scale=0.5)
            TS("dve", tmp_zz, ps_ap, 0.5, None, ALU.mult, r=[ps_key], w=[kzz])
        elif zz_on_act:
            ACT(tmp_th, ps_ap, AF.Tanh, r=[ps_key], w=[kth], scale=0.5, bias=bias_half)
            ACT(tmp_zz, ps_ap, AF.Identity, r=[ps_key], w=[kzz], scale=0.5, bias=bias_half)
        else:
            ACT(tmp_th, ps_ap, AF.Tanh, r=[ps_key], w=[kth], scale=0.5, bias=bias_half)
            TS("dve", tmp_zz, ps_ap, bias_full, 0.5, ALU.add, ALU.mult, r=[ps_key], w=[kzz])
        STT("dve", out_ap, tmp_th, 1.0, tmp_zz, ALU.add, ALU.mult, r=[kth, kzz], w=[out_key])

    def ps(name, shape, dt=F32):
        return es.enter_context(nc.psum_tensor(name, list(shape), dt))

    psA = [ps("psA%d" % i, [128, 512]) for i in range(3)]
    psO = ps("psO", [128, 512])
    psB = ps("psB", [128, 512])
    psC = ps("psC", [128, 512])
    psD = ps("psD", [128, 512])
    psE = ps("psE", [128, 512])

    ident_f = sb("ident_f", [128, 128])
    ident_b = sb("ident_b", [128, 128], BF16)
    ones_f = sb("ones_f", [128, 128])
    ones_b = sb("ones_b", [128, 128], BF16)
    vecs = sb("vecs_sb", [128, NV])
    tri_f = sb("tri_f", [128, 2, 128])
    ada = sb("ada", [128, 24, 2])
    AB = sb("AB", [128, 2, 2, 8])
    uT = sb("uT", [128, 8, T_ALL], BF16)
    stats = sb("stats", [128, NT, 2])
    ycatT = sb("ycatT", [128, 8, T_LAT], BF16)
    MSET("pool", ones_f[:], 1.0, w=["ones_f"])
    MSET("pool", ones_b[:], 1.0, w=["ones_b"])
    P.add("pool", lambda e: e.affine_select(out=ident_f[:], in_=ones_f[:], pattern=[[-1, 128]],
                                            compare_op=ALU.is_equal, fill=0.0, base=0, channel_multiplier=1),
          r=["ones_f"], w=["ident_f"])
    CP("pool", ident_b[:], ident_f[:], r=["ident_f"], w=["ident_b"])
    P.add("pool", lambda e: e.affine_select(out=tri_f[:, 0, :], in_=ones_f[:], pattern=[[1, 128]],
                                            compare_op=ALU.is_ge, fill=0.0, base=0, channel_multiplier=-1),
          r=["ones_f"], w=[("tri_f", 0)])
    P.add("pool", lambda e: e.affine_select(out=tri_f[:, 1, :], in_=ones_f[:], pattern=[[-1, 128]],
                                            compare_op=ALU.is_ge, fill=0.0, base=0, channel_multiplier=1),
          r=["ones_f"], w=[("tri_f", 1)])
    DMA("sp", vecs[:], vecs_d, w=["vecs"])

    win_v = win_d.rearrange("(c p) n -> p c n", p=128)
    sw = ExitStack()
    wom = sbx(sw, "wom", [128, 8, 512], BF16)
    wzm = sbx(sw, "wzm", [128, 8, 512], BF16)
    st0 = ExitStack()
    cc = sbx(st0, "cc", [128, 8, 2])
    th0 = sbx(st0, "th0", [128, 8, 2])
    scc = sbx(st0, "scc", [128, 8, 2], BF16)
    wada = [sbx(st0, "wada%d" % i, [128, 8, 1024]) for i in range(2)]
    wadab = [sbx(st0, "wadab%d" % i, [128, 8, 1024], BF16) for i in range(2)]
    t1 = sbx(st0, "t1", [128, 8])
    NXB = 3
    xt = [sbx(st0, "xt%d" % i, [128, D]) for i in range(NXB)]
    xn = [sbx(st0, "xn%d" % i, [128, D], BF16) for i in range(NXB)]
    st6 = [sbx(st0, "st6_%d" % i, [128, 2, 6]) for i in range(2)]
    mv = [sbx(st0, "mv%d" % i, [128, 2]) for i in range(2)]
    wada_v = wada_d.rearrange("(c p) n -> p c n", p=128)
    ps_ada = psB[:, 0:48].rearrange("p (a b) -> p a b", b=2)
    ps_t = [psC[:].bitcast(BF16).rearrange("p (c t) -> p c t", t=128),
            psD[:].bitcast(BF16).rearrange("p (c t) -> p c t", t=128)]
    ps_tk = ["psC", "psD"]
    ps_u = [psE[:].bitcast(BF16).rearrange("p (c t) -> p c t", t=128),
            psO[:].bitcast(BF16).rearrange("p (c t) -> p c t", t=128)]
    ps_uk = ["psE", "psO"]
    NACT = 6

    def ada_load(part):
        DMA("sp", wada[part % 2][:], wada_v[:, :, part * 1024:(part + 1) * 1024], w=[("wada", part % 2)])

    def ada_mm(part):
        pb = part % 2
        CP("dve", wadab[pb][:, 0:4, :], wada[pb][:, 0:4, :], r=[("wada", pb)], w=[("wadab", pb, 0)])
        CP("act", wadab[pb][:, 4:8, :], wada[pb][:, 4:8, :], r=[("wada", pb)], w=[("wadab", pb, 1)])
        for n in range(8):
            for k in range(8):
                MM(ps_ada[:, part * 8 + n, :], wadab[pb][:, k, n * 128:(n + 1) * 128], scc[:, k, :],
                   start=(k == 0), stop=(k == 7), r=[("wadab", pb), "scc"], w=["psB"])
        for j in range(2):
            TT("dve", ada[:, part * 8:(part + 1) * 8, j], ps_ada[:, part * 8:(part + 1) * 8, j],
               vecs[:, V_BADA + part * 8:V_BADA + (part + 1) * 8], ALU.add, r=["psB", "vecs"], w=[("ada", part, j)])

    def ln_a(i):
        b = i % NXB
        s2 = i % 2
        src = ctx_d[i * 128:(i + 1) * 128, :] if i < 2 else x_d[(i - 2) * 128:(i - 1) * 128, :]
        DMA("sp", xt[b][:], src, w=[("xt", b)])
        for hh in range(2):
            P.add("dve", lambda e, b=b, hh=hh, s2=s2: e.bn_stats(out=st6[s2][:, hh, :],
                                                               in_=xt[b][:, hh * 512:(hh + 1) * 512]),
                  r=[("xt", b)], w=[("st6", s2, hh)])
        P.add("dve", lambda e, s2=s2: e.bn_aggr(out=mv[s2][:], in_=st6[s2][:]), r=[("st6", s2)], w=[("mv", s2)])
        CP("dve", stats[:, i, 0:1], mv[s2][:, 0:1], r=[("mv", s2)], w=[("stats", i)])
        RSQRT(stats[:, i, 1:2], mv[s2][:, 1:2], LN_EPS, r=[("mv", s2)], w=[("stats", i)])

    def ln_a2(i):
        b = i % NXB
        TS("dve", xn[b][:], xt[b][:], stats[:, i, 0:1], stats[:, i, 1:2], ALU.subtract, ALU.mult,
           r=[("xt", b), ("stats", i)], w=[("xn", b)])

    def ln_b(i):
        b = i % NXB
        pb = i % 2
        j = 1 if i < 2 else 0
        for c in range(8):
            tgt, tk = (ps_t[pb], ps_tk[pb]) if c < NACT else (ps_u[pb], ps_uk[pb])
            P.add("pe", lambda e, b=b, c=c, tgt=tgt: e.transpose(out=tgt[:, c, :],
                                                                in_=xn[b][:, c * 128:(c + 1) * 128],
                                                                identity=ident_b[:]),
                  r=[("xn", b), "ident_b"], w=[tk])
        for c in range(8):
            if c < NACT:
                ACT(uT[:, c, i * 128:(i + 1) * 128], ps_t[pb][:, c, :], AF.Identity, r=[ps_tk[pb], ("AB", j)],
                    w=[("uT", c, i)], scale=AB[:, j, 0, c:c + 1], bias=AB[:, j, 1, c:c + 1])
            else:
                TS("dve", uT[:, c, i * 128:(i + 1) * 128], ps_u[pb][:, c, :], AB[:, j, 0, c:c + 1],
                   AB[:, j, 1, c:c + 1], ALU.mult, ALU.add, r=[ps_uk[pb], ("AB", j)], w=[("uT", c, i)])

    DMA("sp", cc[:], cc_d, w=["cc"])
    ada_load(0)
    ada_load(1)
    ACT(th0[:], cc[:], AF.Tanh, r=["cc"], w=["th0"], scale=0.5)
    TS("dve", th0[:], th0[:], 0.5, 0.5, ALU.mult, ALU.add, r=["th0"], w=["th0"])
    TT("dve", scc[:], th0[:], cc[:], ALU.mult, r=["th0", "cc"], w=["scc"])
    ln_a(0)
    ln_a2(0)
    ln_a(1)
    ln_a2(1)
    ada_mm(0)
    ada_mm(1)
    DMA("pool", wom[:], win_v[:, :, C_OM:C_OM + 512], w=["wom"], sticky=True)
    DMA("pool", wzm[:], win_v[:, :, C_ZM:C_ZM + 512], w=["wzm"], sticky=True)
    for j in range(2):
        TS("dve", t1[:], ada[:, 8:16, j], 1.0, None, ALU.add, r=[("ada", 1, j)], w=["t1"])
        TT("dve", AB[:, j, 0, :], t1[:], vecs[:, V_GIN:V_GIN + 8], ALU.mult, r=["t1", "vecs"], w=[("AB", j, 0)])
        TT("dve", AB[:, j, 1, :], t1[:], vecs[:, V_BIN:V_BIN + 8], ALU.mult, r=["t1", "vecs"], w=[("AB", j, 1)])
        TT("dve", AB[:, j, 1, :], AB[:, j, 1, :], ada[:, 0:8, j], ALU.add,
           r=[("AB", j, 1), ("ada", 0, j)], w=[("AB", j, 1)])
    for i in range(NT):
        if i + 2 < NT:
            ln_a(i + 2)
        ln_b(i)
        if i + 2 < NT:
            ln_a2(i + 2)
        if i == NT - 3:
            ada_load(2)
    ada_mm(2)
    P.barrier()
    st0.close()

    blocks = [(0, 256)] + [(256 + 512 * j, 512) for j in range(4)]

    if stage >= 2:
        sm = ExitStack()
        XW = T_ALL + 4
        xmT = sbx(sm, "xmT", [128, 4, XW], BF16)
        xcT = sbx(sm, "xcT", [128, 4, T_ALL], BF16)
        hbT = sbx(sm, "hbT", [128, 4, T_LAT], BF16)
        wbd = [sbx(sm, "wbd%d" % i, [128, 4, 128], BF16) for i in range(4)]
        hcb = sbx(sm, "hcb", [128, 4])
        tmp_th = sbx(sm, "tmp_th", [128, 512])
        tmp_zz = sbx(sm, "tmp_zz", [128, 512])
        lfhl = sbx(sm, "lfhl", [128, NT, 2, 2, 4], BF16)
        lihl = sbx(sm, "lihl", [128, NT, 2, 2, 4], BF16)
        tri_b = sbx(sm, "tri_b", [128, 2, 128], BF16)
        onesb128 = sbx(sm, "onesb128", [128, 128], BF16)
        sm2 = ExitStack()
        gts = sbx(sm2, "gts", [128, NT, 16])
        bgb = sbx(sm2, "bgb", [128, 16])
        wg = sbx(sm2, "wg", [128, 12, 16], BF16)
        lineg = sbx(sm2, "lineg", [128, NT, 2, 4])
        wsm = [sbx(sm2, "wsm%d" % i, [128, 4, 4]) for i in range(3)]
        BD = sbx(sm2, "BD", [128, 32, 4])
        dcw = sbx(sm2, "dcw", [128, 3, 4, 128], BF16)
        wxm = sbx(sm2, "wxm", [128, 8, 512], BF16)
        wxm32 = sbx(sm2, "wxm32", [128, 8, 512])
        wbdT = sbx(sm2, "wbdT", [128, 12, 128], BF16)
        Gc = sbx(sm2, "Gc", [128, 8, 16], BF16)
        lft = sbx(sm2, "lft", [128, NT, 2, 4])

        def xcol(tau):
            return tau + 1 if tau < 256 else tau + 3

        for kk in range(2):
            DMA("sp", wxm32[:, 4 * kk:4 * kk + 4, :], win_v[:, 4 * kk:4 * kk + 4, C_XM:C_XM + 512], w=[("wxm32", kk)])
            CP("act" if kk == 0 else "dve", wxm[:, 4 * kk:4 * kk + 4, :], wxm32[:, 4 * kk:4 * kk + 4, :],
               r=[("wxm32", kk)], w=[("wxm", kk)])
        MSET("dve", xmT[:, :, 0:1], 0.0, w=["xmT"])
        MSET("dve", xmT[:, :, 257:259], 0.0, w=["xmT"])
        MSET("dve", xmT[:, :, XW - 1:XW], 0.0, w=["xmT"])
        DMA("sp", bgb[:], bgate_d.partition_broadcast(128), w=["bgb"])
        DMA("pool", wg[:], wgate_d.rearrange("(c p) n -> p c n", p=128), w=["wg"])
        for i in range(3):
            DMA("sp", wsm[i][:], wm_d[i].rearrange("(h p) o -> p h o", p=128), w=[("wsm", i)])
        MSET("pool", BD[:], 1.0, w=["BD"])
        P.add("pool", lambda e: e.affine_select(out=BD[:], in_=BD[:], pattern=[[-4, 32], [0, 4]],
                                                compare_op=ALU.is_ge, fill=0.0, base=0, channel_multiplier=1),
              r=["BD"], w=["BD"])
        P.add("pool", lambda e: e.affine_select(out=BD[:], in_=BD[:], pattern=[[4, 32], [0, 4]],
                                                compare_op=ALU.is_ge, fill=0.0, base=3, channel_multiplier=-1),
              r=["BD"], w=["BD"])
        for i in range(3):
            for h in range(4):
                TT("dve", wbd[i][:, h, :].rearrange("p (g o) -> p g o", o=4), BD[:],
                   wsm[i][:, h, :].unsqueeze(1).to_broadcast([128, 32, 4]), ALU.mult,
                   r=["BD", ("wsm", i)], w=[("wbd", i, h)])
        TS("dve", wbd[3][:], wbd[1][:], 128.0 ** -0.5, None, ALU.mult, r=[("wbd", 1)], w=[("wbd", 3)])
        for j in range(3):
            for f in range(4):
                TS("dve", dcw[:, j, f, :], ident_f[:], vecs[:, V_CW + j * 4 + f:V_CW + j * 4 + f + 1], None, ALU.mult,
                   r=["ident_f", "vecs"], w=[("dcw", j, f)])
        TS("dve", hcb[:], vecs[:, V_CB:V_CB + 4], 0.5, None, ALU.mult, r=["vecs"], w=["hcb"])

        CP("pool", tri_b[:], tri_f[:], r=["tri_f"], w=["tri_b"])
        MSET("pool", onesb128[:], 1.0 / 128.0, w=["onesb128"])
        for (t0, N) in blocks:
            for f in range(4):
                pb = f % 2
                for k in range(8):
                    MM(psA[pb][:, 0:N], wxm[:, k, f * 128:(f + 1) * 128], uT[:, k, t0:t0 + N],
                       start=(k == 0), stop=(k == 7), r=["wxm", ("uT", k)], w=["psA%d" % pb])
                c0 = xcol(t0)
                CP("act" if pb == 0 else "dve", xmT[:, f, c0:c0 + N], psA[pb][:, 0:N],
                   r=["psA%d" % pb], w=[("xmT", f)])
        rot = [(psA[0], "psA0"), (psA[1], "psA1"), (psA[2], "psA2"), (psO, "psO"), (psC, "psC"), (psD, "psD"),
               (psE, "psE")]
        rot_i = [0]

        def nextbank():
            b_ = rot[rot_i[0] % len(rot)]
            rot_i[0] += 1
            return b_

        tmp2_th = sbx(sm2, "tmp2_th", [128, 512])
        tmp2_zz = sbx(sm2, "tmp2_zz", [128, 512])
        pT0 = psC[:].bitcast(BF16).rearrange("p (c t) -> p c t", t=128)
        pT1 = psD[:].bitcast(BF16).rearrange("p (c t) -> p c t", t=128)
        for i in range(3):
            for h in range(4):
                idx = i * 4 + h
                tgt, tk = (pT0, "psC") if idx < 8 else (pT1, "psD")
                P.add("pe", lambda e, i=i, h=h, idx=idx, tgt=tgt: e.transpose(out=tgt[:, idx % 8, :],
                                                                            in_=wbd[i][:, h, :],
                                                                            identity=ident_b[:]),
                      r=[("wbd", i, h), "ident_b"], w=[tk])
        CP("act", wbdT[:, 0:8, :], pT0[:, 0:8, :], r=["psC"], w=[("wbdT", 0)])
        CP("dve", wbdT[:, 8:12, :], pT1[:, 0:4, :], r=["psD"], w=[("wbdT", 1)])
        for h in range(4):
            MM(psB[:, h * 16:(h + 1) * 16], wbdT[:, h, :], wg[:, h, :], start=True, stop=False,
               r=[("wbdT", 0), "wg"], w=["psB"])
            MM(psB[:, h * 16:(h + 1) * 16], wbdT[:, 4 + h, :], wg[:, 4 + h, :], start=False, stop=True,
               r=[("wbdT", 0), "wg"], w=["psB"])
            MM(psB[:, 64 + h * 16:64 + (h + 1) * 16], wbdT[:, 8 + h, :], wg[:, 8 + h, :],
               r=[("wbdT", 1), "wg"], w=["psB"])
        CP("act", Gc[:].rearrange("p a n -> p (a n)"), psB[:, 0:128], r=["psB"], w=["Gc"])
        for (t0, N) in blocks:
            c0 = xcol(t0)
            for f in range(4):
                pt_, pk_ = nextbank()
                for j in range(3):
                    MM(pt_[:, 0:N], dcw[:, j, f, :], xmT[:, f, c0 + j - 1:c0 + j - 1 + N],
                       start=(j == 0), stop=(j == 2), r=[("dcw", j, f), ("xmT", f)], w=[pk_])
                if f % 2 == 0:
                    silu_from(pt_[:, 0:N], pk_, xcT[:, f, t0:t0 + N], ("xcT", f), None,
                              tmp_th[:, 0:N], tmp_zz[:, 0:N], bias_full=vecs[:, V_CB + f:V_CB + f + 1],
                              bias_half=hcb[:, f:f + 1], zz_on_act=True)
                else:
                    silu_from(pt_[:, 0:N], pk_, xcT[:, f, t0:t0 + N], ("xcT", f), None,
                              tmp2_th[:, 0:N], tmp2_zz[:, 0:N], bias_full=vecs[:, V_CB + f:V_CB + f + 1],
                              bias_half=hcb[:, f:f + 1], kth="tmp2_th", kzz="tmp2_zz", zz_on_act=True)
            for tt in range(N // 128):
                ti = (t0 + tt * 128) // 128
                pg_, pgk_ = nextbank()
                for h in range(4):
                    MM(pg_[:, 0:16], xcT[:, h, t0 + tt * 128:t0 + (tt + 1) * 128], Gc[:, h, :], start=(h == 0),
                       stop=False, r=[("xcT", h), "Gc"], w=[pgk_])
                for h in range(4):
                    MM(pg_[:, 0:16], xmT[:, h, c0 + tt * 128:c0 + (tt + 1) * 128], Gc[:, 4 + h, :], start=False,
                       stop=(h == 3), r=[("xmT", h), "Gc"], w=[pgk_])
                TT("dve", gts[:, ti, :], pg_[:, 0:16], bgb[:], ALU.add, r=[pgk_, "bgb"], w=[("gts", ti)])
        gts4 = gts[:].rearrange("p t (a b) -> p t a b", b=4)
        for dd in range(2):
            ACT(lft[:, :, dd, :], gts4[:, :, 2 * dd + 1, :], AF.Exp, r=["gts"], w=[("lft", dd)], scale=-1.0)
            ACT(lft[:, :, dd, :], lft[:, :, dd, :], AF.Ln, r=[("lft", dd)], w=[("lft", dd)], bias=1.0)
            TS("dve", gts4[:, :, 2 * dd + 1, :], lft[:, :, dd, :], -1.0, None, ALU.mult, r=[("lft", dd)], w=["gts"])

        for dd in range(2):
            TS("dve", lineg[:, :, dd, :], gts4[:, :, 2 * dd, :], -1.0, None, ALU.mult, r=["gts"], w=[("lineg", dd)])
            CP("dve", lihl[:, :, dd, 0, :], lineg[:, :, dd, :], r=[("lineg", dd)], w=[("lihl", dd, 0)])
            TT("dve", lihl[:, :, dd, 1, :], lineg[:, :, dd, :], lihl[:, :, dd, 0, :], ALU.subtract,
               r=[("lineg", dd), ("lihl", dd, 0)], w=[("lihl", dd, 1)])
            CP("dve", lfhl[:, :, dd, 0, :], gts4[:, :, 2 * dd + 1, :], r=["gts"], w=[("lfhl", dd, 0)])
            TT("dve", lfhl[:, :, dd, 1, :], gts4[:, :, 2 * dd + 1, :], lfhl[:, :, dd, 0, :], ALU.subtract,
               r=["gts", ("lfhl", dd, 0)], w=[("lfhl", dd, 1)])
        P.barrier()
        sm2.close()
        state32 = sbx(sm, "state32", [128, 4, 130])
        statebf = sbx(sm, "statebf", [128, 4, 130], BF16)
        Ebc = [sbx(sm, "Ebc%d" % i, [128, 4, 128]) for i in range(2)]
        wcol = [sbx(sm, "wcol%d" % i, [128, 4]) for i in range(2)]
        Kt = [sbx(sm, "Kt%d" % i, [128, 4, 128], BF16) for i in range(2)]
        Vp = [sbx(sm, "Vp%d" % i, [128, 4, 130], BF16) for i in range(2)]
        Qd = [sbx(sm, "Qd%d" % i, [128, 4, 128], BF16) for i in range(2)]
        KT = [sbx(sm, "KT%d" % i, [128, 4, 128], BF16) for i in range(2)]
        S0m = [sbx(sm, "S0m%d" % i, [128, 4, 128], BF16) for i in range(2)]
        dU = [sbx(sm, "dU%d" % i, [128, 4, 130]) for i in range(2)]
        absd = sbx(sm, "absd", [128, 4, 128])
        cellb = [sbx(sm, "cell%d" % i, [128, 4, 128]) for i in range(2)]
        hb1b = sbx(sm, "hb1b", [128, 4, 128], BF16)
        hb1q = sbx(sm, "hb1q", [128, 4, 128], BF16)
        mm_ = tmp_zz[:].rearrange("p (h t) -> p h t", t=128)
        msq = sbx(sm, "msq", [128, 4, 128])
        hb1q_f = sbx(sm, "hb1q_f", [128, 4, 128])
        soT = [sbx(sm, "soT%d" % i, [128, 4, 512], BF16) for i in range(2)]
        szT = [sbx(sm, "szT%d" % i, [128, 4, 512], BF16) for i in range(2)]
        tmpB_th = sbx(sm, "tmpB_th", [128, 512])
        tmpB_zz = sbx(sm, "tmpB_zz", [128, 512])
        ps4 = lambda t: t[:].rearrange("p (h t) -> p h t", t=128)

        def prep(dd, ti, p):
            lat = ti >= 2
            cs = slice(ti * 128, (ti + 1) * 128)
            xs = slice(xcol(ti * 128), xcol(ti * 128) + 128)
            tcol = 127 if dd == 0 else 0
            for h in range(4):
                for a in range(2):
                    MM(psB[:, h * 128:(h + 1) * 128], lfhl[:, ti, dd, a, h:h + 1].to_broadcast([128, 128]),
                       tri_b[:, dd, :], start=(a == 0), stop=(a == 1), r=["tri_b", ("lfhl", dd)], w=["psB"])
            for a in range(2):
                MM(psC[:, 0:4], tri_b[:, dd, :], lfhl[:, ti, dd, a, :], start=(a == 0), stop=False,
                   r=["tri_b", ("lfhl", dd)], w=["psC"])
            for a in range(2):
                MM(psC[:, 0:4], ident_b[:], lihl[:, ti, dd, a, :], start=False, stop=(a == 1),
                   r=["ident_b", ("lihl", dd)], w=["psC"])
            for h in range(4):
                MM(ps4(psD)[:, h, :], xcT[:, h, cs], wbd[3][:, h, :], r=[("xcT", h), ("wbd", 3)], w=["psD"])
            for h in range(4):
                MM(ps4(psE)[:, h, :], xmT[:, h, xs], wbd[2][:, h, :], r=[("xmT", h), ("wbd", 2)], w=["psE"])
            yield
            ACT(wcol[p][:], psC[:, 0:4], AF.Exp, r=["psC"], w=[("wcol", p)], scale=-1.0)
            ACT(Ebc[p][:].rearrange("p h t -> p (h t)"), psB[:], AF.Exp, r=["psB"], w=[("Ebc", p)])
            ACT(Kt[p][:].rearrange("p h t -> p (h t)"), psD[:], AF.Copy, r=["psD"], w=[("Kt", p)])
            yield
            TT("dve", Vp[p][:, :, 0:128], ps4(psE), wcol[p][:].unsqueeze(2).to_broadcast([128, 4, 128]), ALU.mult,
               r=["psE", ("wcol", p)], w=[("Vp", p, 0)])
            CP("act", Vp[p][:, :, 128:129], wcol[p][:].unsqueeze(2), r=[("wcol", p)], w=[("Vp", p, 1)])
            if lat:
                for h in range(4):
                    MM(ps4(psC)[:, h, :], wbd[0][:, h, :], xcT[:, h, cs], r=[("xcT", h), ("wbd", 0)], w=["psC"])
                for h in range(4):
                    MM(ps4(psB)[:, h, :], wbd[3][:, h, :], xcT[:, h, cs], r=[("xcT", h), ("wbd", 3)], w=["psB"])
            yield
            if lat:
                TT("dve", Qd[p][:].rearrange("p h t -> p (h t)"), psC[:], Ebc[p][:].rearrange("p h t -> p (h t)"),
                   ALU.mult, r=["psC", ("Ebc", p)], w=[("Qd", p)])
                CP("dve" if dd == 1 else "act", KT[p][:].rearrange("p h t -> p (h t)"), psB[:], r=["psB"],
                   w=[("KT", p)])
            for h in range(4):
                pu = (psD if h < 2 else psE)[:, (h % 2) * 256:(h % 2) * 256 + 129]
                MM(pu, Kt[p][:, h, :], Vp[p][:, h, 0:129], r=[("Kt", p), ("Vp", p)], w=["psD" if h < 2 else "psE"])
            yield
            if lat:
                for h in range(4):
                    MM(ps4(psC)[:, h, :], KT[p][:, h, :], Qd[p][:, h, :], r=[("KT", p), ("Qd", p)], w=["psC"])
            for h in range(4):
                pu = (psD if h < 2 else psE)[:, (h % 2) * 256:(h % 2) * 256 + 129]
                ACT(dU[p][:, h, 0:129], pu, AF.Identity, r=["psD" if h < 2 else "psE", ("Ebc", p)],
                    w=[("dU", p, h)], scale=Ebc[p][:, h, tcol:tcol + 1])
            yield
            if lat:
                TT("dve", S0m[p][:], ps4(psC), tri_b[:, dd, :].unsqueeze(1).to_broadcast([128, 4, 128]), ALU.mult,
                   r=["psC", "tri_b"], w=[("S0m", p)])
                yield

        def finish(dd, ti, p):
            lat = ti >= 2
            cs = slice(ti * 128, (ti + 1) * 128)
            tcol = 127 if dd == 0 else 0
            if lat:
                lc = slice((ti - 2) * 128, (ti - 1) * 128)
                for h in range(4):
                    MM(ps4(psA[2])[:, h, :], Vp[p][:, h, 0:128], S0m[p][:, h, :], start=True, stop=False,
                       r=[("Vp", p), ("S0m", p)], w=["psA2"])
                    MM(ps4(psA[2])[:, h, :], statebf[:, h, 0:128], Qd[p][:, h, :], start=False, stop=True,
                       r=["statebf", ("Qd", p)], w=["psA2"])
                for h in range(4):
                    MM(ps4(psO)[:, h, :], Vp[p][:, h, 128:129].to_broadcast([128, 128]), S0m[p][:, h, :],
                       start=True, stop=False, r=[("Vp", p), ("S0m", p)], w=["psO"])
                    MM(ps4(psO)[:, h, :], statebf[:, h, 128:129].to_broadcast([128, 128]), Qd[p][:, h, :],
                       start=False, stop=True, r=["statebf", ("Qd", p)], w=["psO"])
            for h in range(4):
                STT("dve", state32[:, h, 0:129], state32[:, h, 0:129], Ebc[p][:, h, tcol:tcol + 1], dU[p][:, h, 0:129],
                    ALU.mult, ALU.add, r=[("state32", h), ("Ebc", p), ("dU", p, h)], w=[("state32", h)])
            yield
            if lat:
                ACT(absd[:].rearrange("p h t -> p (h t)"), psO[:], AF.Abs, r=["psO"], w=["absd"])
            CP("act", statebf[:], state32[:], r=["state32"], w=["statebf"])
            yield
            if lat:
                TS("dve", absd[:], absd[:], 1.0, None, ALU.max, r=["absd"], w=["absd"])
                yield
                RECIP(absd[:], absd[:], r=["absd"], w=["absd"])
                yield
                if dd == 1:
                    TT("dve", hbT[:, :, lc], ps4(psA[2]), absd[:], ALU.mult, r=["psA2", "absd"], w=[("hbT", ti)])
                else:
                    TT("dve", cellb[p][:], ps4(psA[2]), absd[:], ALU.mult, r=["psA2", "absd"], w=[("cell", p)])
                yield

        def blockproj(b):
            pb = b % 2
            b0 = 256 + b * 512
            for (wt, dst, sig) in ((wom, soT[pb], True), (wzm, szT[pb], False)):
                for f in range(4):
                    pk = "psA0" if f % 2 == 0 else "psA1"
                    pt_ = psA[0] if f % 2 == 0 else psA[1]
                    for k in range(8):
                        MM(pt_[:], wt[:, k, f * 128:(f + 1) * 128], uT[:, k, b0:b0 + 512],
                           start=(k == 0), stop=(k == 7), r=["wom", "wzm", ("uT", k)], w=[pk])
                    if sig:
                        ACT(tmpB_th[:], pt_[:], AF.Tanh, r=[pk], w=["tmpB_th"], scale=0.5)
                        TS("dve", dst[:, f, :], tmpB_th[:], 0.5, 0.5, ALU.mult, ALU.add,
                           r=["tmpB_th"], w=[("soT", pb, f)])
                    else:
                        ACT(tmpB_th[:], pt_[:], AF.Tanh, r=[pk], w=["tmpB_th"], scale=0.5)
                        ACT(tmpB_zz[:], pt_[:], AF.Copy, r=[pk], w=["tmpB_zz"], scale=0.5)
                        STT("dve", dst[:, f, :], tmpB_th[:], 1.0, tmpB_zz[:], ALU.add, ALU.mult,
                            r=["tmpB_th", "tmpB_zz"], w=[("szT", pb, f)])
                    yield

        def epi(dd, ti, p):
            cell = cellb[p]
            cs = slice(ti * 128, (ti + 1) * 128)
            lc = slice((ti - 2) * 128, (ti - 1) * 128)
            lt = ti - 2
            pb = (lt // 4) % 2
            bc_ = slice((lt % 4) * 128, (lt % 4) * 128 + 128)
            TT("dve", cell[:], cell[:], hbT[:, :, lc], ALU.add, r=[("cell", p), ("hbT", ti)], w=[("cell", p)])
            TT("dve", cell[:], cell[:], soT[pb][:, :, bc_], ALU.mult, r=[("cell", p), ("soT", pb)], w=[("cell", p)])
            CP("act", hb1b[:], cell[:], r=[("cell", p)], w=["hb1b"])
            TT("dve", hb1q[:], cell[:], cell[:], ALU.mult, r=[("cell", p)], w=["hb1q"])
            for h in range(4):
                ACT(tmp_th[:, h * 128:(h + 1) * 128], xcT[:, h, cs], AF.Identity, r=[("xcT", h), "vecs"],
                    w=["tmp_th"], scale=vecs[:, V_SKIP + h:V_SKIP + h + 1])
            yield
            MM(psA[0][:], onesb128[:], hb1b[:].rearrange("p h t -> p (h t)"), r=["onesb128", "hb1b"], w=["psA0"])
            MM(psA[1][:], onesb128[:], hb1q[:].rearrange("p h t -> p (h t)"), r=["onesb128", "hb1q"], w=["psA1"])
            CP("act", mm_[:].rearrange("p h t -> p (h t)"), psA[0][:], r=["psA0"], w=["tmp_zz"])
            CP("act", msq[:].rearrange("p h t -> p (h t)"), psA[1][:], r=["psA1"], w=["msq"])
            ACT(hb1q_f[:], mm_[:], AF.Square, r=["tmp_zz"], w=["hb1q_f"])
            yield
            TT("dve", msq[:], msq[:], hb1q_f[:], ALU.subtract, r=["msq", "hb1q_f"], w=["msq"])
            TT("dve", cell[:], cell[:], mm_[:], ALU.subtract, r=[("cell", p), "tmp_zz"], w=[("cell", p)])
            yield
            RSQRT(msq[:], msq[:], LN_EPS, r=["msq"], w=["msq"])
            yield
            TT("dve", msq[:], msq[:], vecs[:, V_MHG:V_MHG + 4].unsqueeze(2).to_broadcast([128, 4, 128]), ALU.mult,
               r=["msq", "vecs"], w=["msq"])
            TT("dve", cell[:], cell[:], msq[:], ALU.mult, r=[("cell", p), "msq"], w=[("cell", p)])
            TT("dve", cell[:], cell[:], tmp_th[:].rearrange("p (h t) -> p h t", t=128), ALU.add,
               r=[("cell", p), "tmp_th"], w=[("cell", p)])
            TT("dve", ycatT[:, 4:8, lc], cell[:], szT[pb][:, :, bc_], ALU.mult, r=[("cell", p), ("szT", pb)],
               w=[("ycatT", 1, lt)])
            yield

        cnt_ = 0
        for dd in (1, 0):
            order = ([0, 1] + list(range(2, NT))) if dd == 0 else ([1, 0] + list(range(NT - 1, 1, -1)))
            MSET("pool", state32[:], 0.0, w=["state32"])
            MSET("pool", statebf[:], 0.0, w=["statebf"])
            par = {}
            for idx, ti in enumerate(order):
                par[ti] = cnt_ % 2
                cnt_ += 1
            for _ in prep(dd, order[0], par[order[0]]):
                pass
            def merge(gens):
                while gens:
                    for g in list(gens):
                        try:
                            next(g)
                        except StopIteration:
                            gens.remove(g)

            for idx, ti in enumerate(order):
                gens = []
                if dd == 0 and idx >= 1 and order[idx - 1] >= 2:
                    gens.append(epi(dd, order[idx - 1], par[order[idx - 1]]))
                gens.append(finish(dd, ti, par[ti]))
                if idx + 1 < len(order):
                    gens.append(prep(dd, order[idx + 1], par[order[idx + 1]]))
                if dd == 0 and idx == 0:
                    gens.append(blockproj(0))
                if dd == 0 and ti >= 2 and (ti - 2) % 4 == 1 and (ti - 2) // 4 + 1 < 4:
                    gens.append(blockproj((ti - 2) // 4 + 1))
                merge(gens)
            if dd == 0:
                merge([epi(dd, order[-1], par[order[-1]])])
        P.barrier()
        sm.close()
        sw.close()

    if stage >= 3:
        swo = ExitStack()
        wout = sbx(swo, "wout", [128, 8, D], BF16)
        sa = ExitStack()
        wqA = sbx(sa, "wqA", [128, 2, 8, 96], BF16)
        wqB = sbx(sa, "wqB", [128, 2, 8, 96], BF16)
        wkv = sbx(sa, "wkv", [128, 8, 128], BF16)
        rope = sbx(sa, "rope_sb", [96, 2, T_LAT])
        qnT = sbx(sa, "qnT", [128, 2, T_LAT], BF16)
        kvnT = sbx(sa, "kvnT", [128, T_ALL], BF16)
        szaT = sbx(sa, "szaT", [128, 4, T_LAT], BF16)
        tmp_th = sbx(sa, "tmp_th2", [128, 512])
        tmp_zz = sbx(sa, "tmp_zz2", [128, 512])
        onesq = sbx(sa, "onesq", [128, 128], BF16)
        onesk = sbx(sa, "onesk", [128, 128], BF16)
        Kh = [sbx(sa, "Kh%d" % i, [96, T_ALL], BF16) for i in range(2)]
        sa2 = ExitStack()
        watt = sbx(sa2, "watt", [128, 8, 960], BF16)
        raw = sbx(sa2, "raw", [128, 3, 512])
        sq = sbx(sa2, "sq", [128, 3, 512], BF16)
        rst = sbx(sa2, "rst", [128, 512])
        win32 = sbx(sa2, "win32", [128, 4, 928])
        wqb32 = sbx(sa2, "wqb32", [128, 2, 768])
        wkvb32 = sbx(sa2, "wkvb32", [128, 1024])
        MSET("dve", onesq[:], 1.0 / 256.0, w=["onesq"])
        MSET("dve", onesk[:], 1.0 / 128.0, w=["onesk"])
        MSET("dve", wqB[:], 0.0, w=["wqB"])
        for kk in range(2):
            ks = slice(4 * kk, 4 * kk + 4)
            DMA("sp", win32[:], win_v[:, ks, 0:928], w=["win32"])
            CP("act", watt[:, ks, 0:416], win32[:, :, 0:416], r=["win32"], w=[("watt", kk, 0)])
            for (dst, srcc) in ((0, 8), (8, 0), (16, 24), (24, 16)):
                CP("dve", watt[:, ks, 416 + dst:416 + dst + 8], win32[:, :, C_KR + srcc:C_KR + srcc + 8],
                   r=["win32"], w=[("watt", kk, 1)])
            CP("dve", watt[:, ks, 448:960], win32[:, :, C_ZA:C_ZA + 512], r=["win32"], w=[("watt", kk, 2)])
            if kk == 0:
                DMA("sp", wkvb32[:], wkvb_d, w=["wkvb32"])
                DMA("sp", wqb32[:], wqb_d.rearrange("(c p) n -> p c n", p=128), w=["wqb32"])
                for a in range(2):
                    DMA("sp", rope[64:96, a, :], rope_d[a], w=["rope"])
        wqb32v = wqb32[:].rearrange("p c (h n) -> p c h n", n=96)
        for c in range(2):
            CP("act", wqA[:, c, :, :], wqb32v[:, c, :, :], r=["wqb32"], w=["wqA"])
            for (dst, srcc) in ((0, 8), (8, 0), (16, 24), (24, 16)):
                CP("dve", wqB[:, c, :, 64 + dst:64 + dst + 8], wqb32v[:, c, :, 64 + srcc:64 + srcc + 8],
                   r=["wqb32"], w=["wqB"])
        CP("act", wkv[:], wkvb32[:].rearrange("p (h n) -> p h n", n=128), r=["wkvb32"], w=["wkv"])
        rotA = [(psA[0], "psA0"), (psA[1], "psA1"), (psA[2], "psA2"), (psO, "psO"), (psC, "psC"), (psD, "psD"),
                (psE, "psE")]
        rotA_i = [0]

        def nextbankA():
            b_ = rotA[rotA_i[0] % len(rotA)]
            rotA_i[0] += 1
            return b_

        rst2 = sbx(sa2, "rst2", [128, 512])
        for bi, (t0, N) in enumerate(blocks):
            lat = bi > 0
            l0 = t0 - 256
            for c in ((2, 0, 1) if lat else (2,)):
                pt_, pk_ = nextbankA()
                for k in range(8):
                    MM(pt_[:, 0:N], watt[:, k, c * 128:(c + 1) * 128], uT[:, k, t0:t0 + N],
                       start=(k == 0), stop=(k == 7), r=["watt", ("uT", k)], w=[pk_])
                CP("act", raw[:, c, 0:N], pt_[:, 0:N], r=[pk_], w=[("raw", c)])
                TT("dve", sq[:, c, 0:N], raw[:, c, 0:N], raw[:, c, 0:N], ALU.mult, r=[("raw", c)], w=[("sq", c)])
            ptA, pkA = nextbankA()
            for k in range(8):
                MM(ptA[0:96, 0:N], watt[:, k, 320:416], uT[:, k, t0:t0 + N], start=(k == 0), stop=(k == 7),
                   r=["watt", ("uT", k)], w=[pkA])
            if lat:
                ptB, pkB = nextbankA()
                for k in range(8):
                    MM(ptB[0:96, 0:N], watt[:, k, 352:448], uT[:, k, t0:t0 + N], start=(k == 0), stop=(k == 7),
                       r=["watt", ("uT", k)], w=[pkB])
            pt_, pk_ = nextbankA()
            MM(pt_[:, 0:N], onesk[:], sq[:, 2, 0:N], r=["onesk", ("sq", 2)], w=[pk_])
            RSQRT(rst2[:, 0:N], pt_[:, 0:N], RMS_EPS, r=[pk_], w=["rst2"])
            STT("dve", kvnT[:, t0:t0 + N], raw[:, 2, 0:N], vecs[:, V_GKVA:V_GKVA + 1], rst2[:, 0:N],
                ALU.mult, ALU.mult, r=[("raw", 2), "vecs", "rst2"], w=["kvnT"])
            if lat:
                for c in range(2):
                    MM(psB[:, 0:N], onesq[:], sq[:, c, 0:N], start=(c == 0), stop=(c == 1), r=["onesq", ("sq", c)],
                       w=["psB"])
                RSQRT(rst[:, 0:N], psB[:, 0:N], RMS_EPS, r=["psB"], w=["rst"])
                for c in range(2):
                    STT("dve", qnT[:, c, l0:l0 + N], raw[:, c, 0:N], vecs[:, V_GQA + c:V_GQA + c + 1], rst[:, 0:N],
                        ALU.mult, ALU.mult, r=[("raw", c), "vecs", "rst"], w=[("qnT", c)])
                TT("dve", tmp_th[64:96, 0:N], ptA[64:96, 0:N], rope[64:96, 0, l0:l0 + N], ALU.mult,
                   r=[pkA, "rope"], w=["tmp_th"])
                TT("dve", tmp_zz[64:96, 0:N], ptB[64:96, 0:N], rope[64:96, 1, l0:l0 + N], ALU.mult,
                   r=[pkB, "rope"], w=["tmp_zz"])
                TT("dve", Kh[0][64:96, t0:t0 + N], tmp_th[64:96, 0:N], tmp_zz[64:96, 0:N], ALU.add,
                   r=["tmp_th", "tmp_zz"], w=[("Kh", 0, "r")])
                CP("act", Kh[1][64:96, t0:t0 + N], Kh[0][64:96, t0:t0 + N], r=[("Kh", 0, "r")], w=[("Kh", 1, "r")])
                for f in range(4):
                    pt_, pk_ = nextbankA()
                    for k in range(8):
                        MM(pt_[:, 0:N], watt[:, k, 448 + f * 128:448 + (f + 1) * 128], uT[:, k, t0:t0 + N],
                           start=(k == 0), stop=(k == 7), r=["watt", ("uT", k)], w=[pk_])
                    if f % 2 == 0:
                        silu_from(pt_[:, 0:N], pk_, szaT[:, f, l0:l0 + N], ("szaT", f), None,
                                  tmp_th[:, 0:N], tmp_zz[:, 0:N])
                    else:
                        silu_from(pt_[:, 0:N], pk_, szaT[:, f, l0:l0 + N], ("szaT", f), None,
                                  rst2[:, 0:N], rst[:, 0:N], kth="rst2", kzz="rst")
            else:
                CP("act", Kh[0][64:96, t0:t0 + N], ptA[64:96, 0:N], r=[pkA], w=[("Kh", 0, "r")])
                CP("dve", Kh[1][64:96, t0:t0 + N], ptA[64:96, 0:N], r=[pkA], w=[("Kh", 1, "r")])

        P.barrier()
        sa2.close()
        Qh = [sbx(sa, "Qh%d" % i, [96, T_LAT], BF16) for i in range(2)]
        Vh = [sbx(sa, "Vh%d" % i, [128, NT, 128], BF16) for i in range(2)]
        PT = [sbx(sa, "PT%d" % i, [128, 512], BF16) for i in range(4)]
        rden = [sbx(sa, "rden%d" % i, [128, 512]) for i in range(2)]
        gg = [sbx(sa, "gg%d" % i, [128, 512]) for i in range(2)]
        rhl = [sbx(sa, "rhl%d" % i, [128, 2, 512], BF16) for i in range(2)]
        DMA("pool", wout[:], wout_d.rearrange("(c p) n -> p c n", p=128), w=["wout"])
        MSET("dve", Vh[0][:], 0.0, w=[("Vh", 0)])
        MSET("dve", Vh[1][:], 0.0, w=[("Vh", 1)])
        MSET("dve", Vh[0][:, :, 64:65], 1.0, w=[("Vh", 0)])
        MSET("dve", Vh[1][:, :, 0:1], 1.0, w=[("Vh", 1)])
        def prologue(h):
            hb_ = h % 2
            voff = 0 if hb_ == 0 else 64
            for bi, (t0, N) in enumerate(blocks):
                MM(psC[0:64, 0:N], wkv[:, h, 0:64], kvnT[:, t0:t0 + N], r=["wkv", "kvnT"], w=["psC"])
                CP("dve", Kh[hb_][0:64, t0:t0 + N], psC[0:64, 0:N], r=["psC"], w=[("Kh", hb_, "n")])
                yield
            for g0 in range(0, NT, 8):
                nt_ = min(8, NT - g0)
                pv = psD[:].rearrange("p (j n) -> p j n", n=64)
                for jj in range(nt_):
                    MM(pv[:, jj, :], kvnT[:, (g0 + jj) * 128:(g0 + jj + 1) * 128], wkv[:, h, 64:128],
                       r=["wkv", "kvnT"], w=["psD"])
                CP("dve", Vh[hb_][:, g0:g0 + nt_, voff:voff + 64], pv[:, 0:nt_, :], r=["psD"], w=[("Vh", hb_)])
                yield
            for qb in range(4):
                qs = slice(qb * 512, (qb + 1) * 512)
                for c in range(2):
                    MM(psC[0:96, :], wqA[:, c, h, :], qnT[:, c, qs], start=(c == 0), stop=(c == 1),
                       r=["wqA", ("qnT", c)], w=["psC"])
                yield
                for c in range(2):
                    MM(psD[0:96, :], wqB[:, c, h, :], qnT[:, c, qs], start=(c == 0), stop=(c == 1),
                       r=["wqB", ("qnT", c)], w=["psD"])
                CP("dve", Qh[hb_][0:64, qs], psC[0:64, :], r=["psC"], w=[("Qh", hb_, qb)])
                TT("dve", tmp_th[64:96, :], psC[64:96, :], rope[64:96, 0, qs], ALU.mult, r=["psC", "rope"],
                   w=["tmp_th"])
                TT("dve", tmp_zz[64:96, :], psD[64:96, :], rope[64:96, 1, qs], ALU.mult, r=["psD", "rope"],
                   w=["tmp_zz"])
                TT("dve", Qh[hb_][64:96, qs], tmp_th[64:96, :], tmp_zz[64:96, :], ALU.add,
                   r=["tmp_th", "tmp_zz"], w=[("Qh", hb_, qb)])
                yield

        SK = 3
        G = 8 * 4 * NT
        fire = []
        pro = None
        for _ in prologue(0):
            pass
        for gs in range(G + SK):
            if gs < G:
                blk, kt = divmod(gs, NT)
                h, qb = divmod(blk, 4)
                hb_ = h % 2
                if blk % 4 == 0 and kt == 0:
                    if pro is not None:
                        for _ in pro:
                            pass
                    pro = prologue(h + 1) if h < 7 else None
                qs = slice(qb * 512, (qb + 1) * 512)
                sbk = gs % 3
                MM(psA[sbk][:], Kh[hb_][:, kt * 128:(kt + 1) * 128], Qh[hb_][:, qs],
                   r=[("Kh", hb_), ("Qh", hb_, qb)], w=["psA%d" % sbk])
                ACT(PT[gs % 4][:], psA[sbk][:], AF.Exp, r=["psA%d" % sbk], w=[("PT", gs % 4)], scale=ATT_SCALE)
            if gs >= SK:
                g2 = gs - SK
                blk2, kt2 = divmod(g2, NT)
                h2, qb2 = divmod(blk2, 4)
                hb2 = h2 % 2
                M2 = 65 if hb2 == 0 else 128
                ob = blk2 % 2
                pso = psO if ob == 0 else psE
                pok = "psO" if ob == 0 else "psE"
                MM(pso[0:M2, :], Vh[hb2][:, kt2, 0:M2], PT[g2 % 4][:], start=(kt2 == 0), stop=(kt2 == NT - 1),
                   r=[("Vh", hb2), ("PT", g2 % 4)], w=[pok])
                if kt2 == NT - 1:
                    pd = 64 if hb2 == 0 else 0
                    r0 = 0 if hb2 == 0 else 64
                    qs2 = slice(qb2 * 512, (qb2 + 1) * 512)

                    def ep1(ob=ob, pso=pso, pok=pok, pd=pd):
                        RECIP(rden[ob][pd:pd + 1, :], pso[pd:pd + 1, :], r=[pok], w=[("rden", ob)])
                        CP("dve", rhl[ob][pd:pd + 1, 0, :], rden[ob][pd:pd + 1, :], r=[("rden", ob)], w=[("rhl", ob)])
                        TT("dve", rhl[ob][pd:pd + 1, 1, :], rden[ob][pd:pd + 1, :], rhl[ob][pd:pd + 1, 0, :],
                           ALU.subtract, r=[("rden", ob), ("rhl", ob)], w=[("rhl", ob)])

                    def ep2(ob=ob, pso=pso, pok=pok, pd=pd, r0=r0, qs2=qs2, h2=h2, qb2=qb2):
                        MM(psB[:], ones_b[pd:pd + 1, :], rhl[ob][pd:pd + 1, 0, :], start=True, stop=False,
                           r=["ones_b", ("rhl", ob)], w=["psB"])
                        MM(psB[:], ones_b[pd:pd + 1, :], rhl[ob][pd:pd + 1, 1, :], start=False, stop=True,
                           r=["ones_b", ("rhl", ob)], w=["psB"])
                        TT("dve", gg[ob][r0:r0 + 64, :], psB[r0:r0 + 64, :], szaT[r0:r0 + 64, h2 // 2, qs2], ALU.mult,
                           r=["psB", ("szaT", h2 // 2)], w=[("gg", ob)])
                        TT("dve", ycatT[r0:r0 + 64, h2 // 2, qs2], pso[r0:r0 + 64, :], gg[ob][r0:r0 + 64, :], ALU.mult,
                           r=[pok, ("gg", ob)], w=[("ycatT", 0, h2, qb2)])
                    fire.append((gs + 3, ep1))
                    fire.append((gs + 8, ep2))
            if pro is not None and gs % 4 == 1:
                next(pro, None)
            while fire and fire[0][0] <= gs:
                fire.pop(0)[1]()
        while fire:
            fire.pop(0)[1]()
        P.barrier()
        sa.close()

    if stage >= 4:
        so = ExitStack()
        gate_bc = sbx(so, "gate_bc", [128, D])
        dg = sbx(so, "dg", [128, 2, 128])
        for c in range(8):
            TS("dve", dg[:, c % 2, :], ident_f[:], ada[:, 16 + c, 0:1], None, ALU.mult, r=["ident_f", ("ada", 2, 0)],
               w=[("dg", c % 2)])
            MM(psA[c % 2][:, 0:128], ones_f[:], dg[:, c % 2, :], r=["ones_f", ("dg", c % 2)], w=["psA%d" % (c % 2)])
            CP("act", gate_bc[:, c * 128:(c + 1) * 128], psA[c % 2][:, 0:128], r=["psA%d" % (c % 2)], w=[("gate_bc", c)])
        rowsb = sbx(so, "rowsb", [128, 4, D])
        xin = [sbx(so, "xin%d" % i, [128, D]) for i in range(3)]
        NPRE = 2
        for a in range(2):
            DMA("sp", rowsb[:, a, :], rows_d[a:a + 1, :].partition_broadcast(128), w=[("rowsb", a)])
        for lt in range(NPRE):
            DMA("sp", xin[lt % 3][:], x_d[lt * 128:(lt + 1) * 128, :], w=[("xin", lt % 3)])
        for a in range(2, 4):
            DMA("sp", rowsb[:, a, :], rows_d[a:a + 1, :].partition_broadcast(128), w=[("rowsb", a)])
        for a in range(2):
            TS("dve", rowsb[:, a, :], rowsb[:, a, :], ALPHA, None, ALU.mult, r=[("rowsb", a)], w=[("rowsb", a)])
        for c in range(8):
            cs_ = slice(c * 128, (c + 1) * 128)
            TT("dve", wout[:, :, cs_], wout[:, :, cs_], gate_bc[:, cs_].unsqueeze(1).to_broadcast([128, 8, 128]),
               ALU.mult, r=["wout", ("gate_bc", c)], w=["wout"])
        bhl = sbx(so, "bhl", [1, 2, D], BF16)
        CP("dve", bhl[0:1, 0, :], rowsb[0:1, 1, :], r=[("rowsb", 1)], w=["bhl"])
        TT("dve", bhl[0:1, 1, :], rowsb[0:1, 1, :], bhl[0:1, 0, :], ALU.subtract, r=[("rowsb", 1), "bhl"], w=["bhl"])
        nmr = sbx(so, "nmr", [128, NT])
        STT("dve", nmr[:], stats[:, :, 0], -1.0, stats[:, :, 1], ALU.mult, ALU.mult, r=["stats"], w=["nmr"])
        nmo = sbx(so, "nmo", [128, 1])
        xo = [sbx(so, "xo%d" % i, [128, D]) for i in range(2)]
        pre = [sbx(so, "pre%d" % i, [128, D]) for i in range(2)]
        st6o = sbx(so, "st6o", [128, 2, 6])
        mvo = sbx(so, "mvo", [128, 2])

        def prefetch(lt):
            if lt + NPRE < 16:
                nb = (lt + NPRE) % 3
                DMA("sp", xin[nb][:], x_d[(lt + NPRE) * 128:(lt + NPRE + 1) * 128, :], w=[("xin", nb)])

        def stage1(lt):
            xb = lt % 3
            ti = lt + 2
            ACT(xin[xb][:], xin[xb][:], AF.Identity, r=[("xin", xb), "nmr"], w=[("xin", xb)],
                scale=stats[:, ti, 1:2], bias=nmr[:, ti:ti + 1])
            TT("dve", xin[xb][:], xin[xb][:], rowsb[:, 0, :], ALU.mult, r=[("xin", xb), ("rowsb", 0)],
               w=[("xin", xb)])

        prefetch(0)
        stage1(0)
        for lt in range(16):
            b = lt % 2
            xb = lt % 3
            cs = slice(lt * 128, (lt + 1) * 128)
            if lt + 1 < 16:
                stage1(lt + 1)
            bankset = [((psA[0], "psA0"), (psA[1], "psA1")), ((psA[2], "psA2"), (psO, "psO")),
                       ((psC, "psC"), (psD, "psD"))][lt % 3]
            for hf in range(2):
                hs = slice(hf * 512, (hf + 1) * 512)
                pt_, pk_ = bankset[hf]
                for k in range(8):
                    MM(pt_[:], ycatT[:, k, cs], wout[:, k, hs], start=(k == 0), stop=False,
                       r=["ycatT", ("wout", k)], w=[pk_])
                for a in range(2):
                    MM(pt_[:], ones_b[0:1, :], bhl[0:1, a, hs], start=False, stop=(a == 1),
                       r=["ones_b", "bhl"], w=[pk_])
            for hf in range(2):
                hs = slice(hf * 512, (hf + 1) * 512)
                pt_, pk_ = bankset[hf]
                TT("dve", pre[b][:, hs], pt_[:], xin[xb][:, hs], ALU.add, r=[pk_, ("xin", xb)],
                   w=[("pre", b, hf)])
                P.add("dve", lambda e, b=b, hf=hf, hs=hs: e.bn_stats(out=st6o[:, hf, :], in_=pre[b][:, hs]),
                      r=[("pre", b, hf)], w=[("st6o", hf)])
            P.add("dve", lambda e: e.bn_aggr(out=mvo[:], in_=st6o[:]), r=["st6o"], w=["mvo"])
            RSQRT(mvo[:, 1:2], mvo[:, 1:2], LN_EPS, r=["mvo"], w=["mvo"])
            STT("dve", nmo[:], mvo[:, 0:1], -1.0, mvo[:, 1:2], ALU.mult, ALU.mult, r=["mvo"], w=["nmo"])
            if lt + 1 < 16:
                prefetch(lt + 1)
            for hf in range(2):
                hs = slice(hf * 512, (hf + 1) * 512)
                ACT(pre[b][:, hs], pre[b][:, hs], AF.Identity, r=[("pre", b, hf), "mvo", "nmo"], w=[("pre", b, hf)],
                    scale=mvo[:, 1:2], bias=nmo[:])
            for hf in range(2):
                hs = slice(hf * 512, (hf + 1) * 512)
                TT("dve", pre[b][:, hs], pre[b][:, hs], rowsb[:, 2, hs], ALU.mult, r=[("pre", b, hf), ("rowsb", 2)],
                   w=[("pre", b, hf)])
                TT("dve", xo[b][:, hs], pre[b][:, hs], rowsb[:, 3, hs], ALU.add, r=[("pre", b, hf), ("rowsb", 3)],
                   w=[("xo", b, hf)])
            DMA("sp", out_d[cs, :], xo[b][:], r=[("xo", b)], w=[("outd", lt)])
        P.add("sp", lambda e: e.nop(), r=["outd"])
        P.barrier()
        so.close()
        swo.close()

    P.barrier()
    info = P.emit()
    es.close()
    return nc, info


def _rope_tables():
    n_rows = T_LAT // 64
    row = np.repeat(np.arange(n_rows, dtype=np.float32), 64)
    col = np.tile(np.arange(64, dtype=np.float32), n_rows)
    inv = (np.float32(10000.0) ** (-np.arange(8, dtype=np.float32) / np.float32(8))).astype(np.float32)
    ang = np.stack([row[:, None] * inv, col[:, None] * inv], axis=1).astype(np.float32)
    cos = np.cos(ang).astype(np.float32)
    sin = np.sin(ang).astype(np.float32)
    tab = np.zeros((2, 32, T_LAT), np.float32)
    for ax in range(2):
        for half in range(2):
            for f in range(8):
                r = ax * 16 + half * 8 + f
                tab[0, r] = cos[:, ax, f]
                tab[1, r] = -sin[:, ax, f] if half == 0 else sin[:, ax, f]
    return tab


def _fm(v):
    return np.asarray(v, np.float32).reshape(-1, 128).T


def make_in_maps(inp, ncores=8):
    rope = _rope_tables()
    rows = np.ascontiguousarray(np.stack([inp["ln_in_g"], inp["ln_in_b"], inp["ln_g"][0], inp["ln_b"][0]]).astype(np.float32))
    vecs = np.ascontiguousarray(np.concatenate(
        [_fm(inp["ln_in_g"]), _fm(inp["ln_in_b"]), _fm(inp["b_ada"][0]), _fm(inp["conv_w"][0]), _fm(inp["conv_b"][0]),
         _fm(inp["mh_g"][0]), _fm(inp["skip"][0]), _fm(inp["g_qa"][0]), _fm(inp["g_kva"][0])], axis=1).astype(np.float32))
    assert vecs.shape == (128, NV)
    maps = []
    for b in range(ncores):
        cc = np.stack([inp["c"][b], inp["c_ctx"]], axis=-1)
        ccT = np.ascontiguousarray(cc.reshape(8, 128, 2).transpose(1, 0, 2)).astype(np.float32)
        maps.append(dict(
            x=np.ascontiguousarray(inp["x"][b]), ctx=np.ascontiguousarray(inp["ctx"][b]), ccT=ccT, vecs=vecs,
            w_ada=np.ascontiguousarray(inp["w_ada"][0]), w_in=np.ascontiguousarray(inp["w_in"][0]),
            w_qb=np.ascontiguousarray(inp["w_qb"][0]), w_kvb=np.ascontiguousarray(inp["w_kvb"][0]),
            w_out=np.ascontiguousarray(inp["w_out"][0]), w_gate=np.ascontiguousarray(inp["w_gate"][0]),
            b_gate=np.ascontiguousarray(inp["b_gate"][0].reshape(1, 16)),
            w_mq=np.ascontiguousarray(inp["w_mq"][0].reshape(512, 4)),
            w_mk=np.ascontiguousarray(inp["w_mk"][0].reshape(512, 4)),
            w_mv=np.ascontiguousarray(inp["w_mv"][0].reshape(512, 4)),
            rows=rows, rope=rope))
    return maps


_NC_CACHE = {}


def kernel(**inputs):
    inp = {k: np.asarray(v) for k, v in inputs.items()}
    if "nc" not in _NC_CACHE:
        _NC_CACHE["nc"] = build_nc()[0]
    nc = _NC_CACHE["nc"]
    maps = make_in_maps(inp, 8)
    res = run_bass_kernel_spmd(nc, maps, core_ids=list(range(8)))
    out = np.stack([np.asarray(r["out"], dtype=np.float32) for r in res.results], axis=0)
    return out
```
